# Optimizing a Trainium2 kernel written in Bass

```python
import math
import jax, jax.numpy as jnp
from jax import lax
import numpy as np

D_MODEL = 1024
BATCH = 8
SEQ = 4096
DEPTH = 2

DA_HEADS = 4
DA_QK_DIM = 64
DA_V_DIM = 2 * DA_QK_DIM
RET_HEADS = 4
RET_QK_DIM = 64
RET_V_DIM = 64
RET_CHUNK = 128
SC_WIDTH = 256
SC_GROUPS = 4
CONV_W = 3
D_FF = 2816
REL_BUCKETS = 32
REL_MAX_DIST = 128
ROPE_THETA = 10000.0
Q_BLOCK = 128
EPS = 1e-6
N_BRANCH = 3

DA_QK_W = DA_HEADS * 2 * DA_QK_DIM
DA_V_W = DA_HEADS * DA_V_DIM
RET_QK_W = RET_HEADS * RET_QK_DIM
RET_V_W = RET_HEADS * RET_V_DIM
IN_SPLITS = (DA_QK_W, DA_QK_W, DA_V_W,
             RET_QK_W, RET_QK_W, RET_V_W, RET_V_W,
             SC_WIDTH, SC_WIDTH, SC_WIDTH,
             N_BRANCH * D_MODEL)
IN_WIDTH = 2 * DA_QK_W + DA_V_W + 2 * RET_QK_W + 2 * RET_V_W + 3 * SC_WIDTH + N_BRANCH * D_MODEL

kernel_name = "hybrid_gated_diffattn_retention_shortconv"


def rmsnorm(x, g):
    xf = x.astype(jnp.float32)
    y = xf * lax.rsqrt(jnp.mean(xf * xf, axis=-1, keepdims=True) + EPS)
    return (y * g.astype(jnp.float32)).astype(x.dtype)


def causal_dwconv(x, w, b):
    S = x.shape[1]
    K = w.shape[0]
    xp = jnp.pad(x, ((0, 0), (K - 1, 0), (0, 0)))
    y = b + xp[:, 0:S] * w[0]
    for k in range(1, K):
        y = y + xp[:, k:k + S] * w[k]
    return y


def rel_bucket(n):
    max_exact = REL_BUCKETS // 2
    nf = jnp.maximum(n, 1).astype(jnp.float32)
    large = max_exact + (jnp.log(nf / max_exact) / math.log(REL_MAX_DIST / max_exact)
                         * (REL_BUCKETS - max_exact)).astype(jnp.int32)
    large = jnp.minimum(large, REL_BUCKETS - 1)
    return jnp.where(n < max_exact, n, large)


def rotary(x, pos):
    d = x.shape[-1]
    half = d // 2
    inv = ROPE_THETA ** (-jnp.arange(half, dtype=jnp.float32) / half)
    ang = pos.astype(jnp.float32)[:, None] * inv[None, :]
    cos = jnp.cos(ang)[None, :, None, :]
    sin = jnp.sin(ang)[None, :, None, :]
    xf = x.astype(jnp.float32)
    x1, x2 = xf[..., :half], xf[..., half:]
    return jnp.concatenate([x1 * cos - x2 * sin, x2 * cos + x1 * sin], axis=-1)


def diff_attention(q, k, v, lam, bias_dist):
    B, S, H = q.shape[0], q.shape[1], q.shape[2]
    qh = jnp.transpose(q, (0, 2, 3, 1, 4)).astype(jnp.float32)
    kh = jnp.transpose(k, (0, 2, 3, 1, 4)).astype(jnp.float32)
    vh = jnp.transpose(v, (0, 2, 1, 3)).astype(jnp.float32)
    kpos = jnp.arange(S)
    neg = jnp.finfo(jnp.float32).min

    def block(i):
        start = i * Q_BLOCK
        qb = lax.dynamic_slice_in_dim(qh, start, Q_BLOCK, axis=3)
        dist = (start + jnp.arange(Q_BLOCK))[:, None] - kpos[None, :]
        bias = jnp.transpose(bias_dist[jnp.maximum(dist, 0)], (2, 0, 1)).astype(jnp.float32)
        s = jnp.einsum('bhmqd,bhmkd->bhmqk', qb, kh) + bias[None, :, None]
        s = jnp.where(dist >= 0, s, neg)
        p = jax.nn.softmax(s, axis=-1)
        a = p[:, :, 0] - lam * p[:, :, 1]
        return jnp.einsum('bhqk,bhke->bhqe', a, vh)

    out = lax.map(block, jnp.arange(S // Q_BLOCK))
    return jnp.transpose(out, (1, 0, 3, 2, 4)).reshape(B, S, H, -1)


def retention(q, k, v):
    B, S, H, dk = q.shape
    dv = v.shape[-1]
    C = RET_CHUNK
    N = S // C
    lg = jnp.log(1.0 - 2.0 ** (-5.0 - jnp.arange(H, dtype=jnp.float32)))
    to_chunks = lambda t: jnp.transpose(t.astype(jnp.float32).reshape(B, N, C, H, t.shape[-1]), (0, 3, 1, 2, 4))
    qc, kc, vc = to_chunks(q), to_chunks(k), to_chunks(v)
    idx = jnp.arange(C, dtype=jnp.float32)
    diff = idx[:, None] - idx[None, :]
    d_intra = jnp.where(diff[None] >= 0, jnp.exp(lg[:, None, None] * jnp.maximum(diff, 0.0)[None]), 0.0)
    inner = jnp.einsum('bhncd,bhnjd->bhncj', qc, kc) * d_intra[None, :, None]
    inner = jnp.einsum('bhncj,bhnje->bhnce', inner, vc)
    k_dec = kc * jnp.exp(lg[:, None] * (C - 1 - idx)[None])[None, :, None, :, None]
    kv = jnp.einsum('bhncd,bhnce->nbhde', k_dec, vc)
    chunk_decay = jnp.exp(lg * C)[None, :, None, None]

    def step(R, kv_n):
        return R * chunk_decay + kv_n, R

    _, R_prev = lax.scan(step, jnp.zeros((B, H, dk, dv), jnp.float32), kv)
    q_dec = qc * jnp.exp(lg[:, None] * (idx + 1.0)[None])[None, :, None, :, None]
    cross = jnp.einsum('bhncd,nbhde->bhnce', q_dec, R_prev)
    o = inner + cross
    return jnp.transpose(o, (0, 2, 3, 1, 4)).reshape(B, S, H, dv)


def setup_inputs(seed: int = 0) -> dict:
    key = jax.random.key(seed)
    ks = jax.random.split(key, 24)
    f32 = jnp.float32
    nrm = lambda k, shape, s: jax.random.normal(k, shape, f32) * s
    gain = lambda k, shape: 1.0 + 0.05 * jax.random.normal(k, shape, f32)
    return {
        "x": nrm(ks[0], (BATCH, SEQ, D_MODEL), 1.0),
        "rel_bias": nrm(ks[1], (REL_BUCKETS, DA_HEADS), 0.5),
        "norm_mix_g": gain(ks[2], (DEPTH, D_MODEL)),
        "w_in": nrm(ks[3], (DEPTH, D_MODEL, IN_WIDTH), D_MODEL ** -0.5),
        "b_gate": nrm(ks[4], (DEPTH, N_BRANCH * D_MODEL), 0.02),
        "da_q_norm_g": gain(ks[5], (DEPTH, DA_QK_DIM)),
        "da_k_norm_g": gain(ks[6], (DEPTH, DA_QK_DIM)),
        "da_lambda": nrm(ks[7], (DEPTH, 4, DA_QK_DIM), 0.1),
        "da_subln_g": gain(ks[8], (DEPTH, DA_V_DIM)),
        "ret_norm_g": gain(ks[9], (DEPTH, RET_V_DIM)),
        "sc_conv_w": nrm(ks[10], (DEPTH, CONV_W, SC_WIDTH), CONV_W ** -0.5),
        "sc_conv_b": nrm(ks[11], (DEPTH, SC_WIDTH), 0.02),
        "w_branch_da": nrm(ks[12], (DEPTH, DA_V_W, D_MODEL), DA_V_W ** -0.5),
        "w_branch_ret": nrm(ks[13], (DEPTH, RET_V_W, D_MODEL), RET_V_W ** -0.5),
        "w_branch_sc": nrm(ks[14], (DEPTH, SC_WIDTH, D_MODEL), SC_WIDTH ** -0.5),
        "w_out": nrm(ks[15], (DEPTH, D_MODEL, D_MODEL), 0.5 * D_MODEL ** -0.5),
        "norm_ffn_g": gain(ks[16], (DEPTH, D_MODEL)),
        "w_ffn_in": nrm(ks[17], (DEPTH, D_MODEL, 2 * D_FF), D_MODEL ** -0.5),
        "ffn_conv_w": nrm(ks[18], (DEPTH, CONV_W, D_FF), CONV_W ** -0.5),
        "ffn_conv_b": nrm(ks[19], (DEPTH, D_FF), 0.02),
        "w_ffn_out": nrm(ks[20], (DEPTH, D_FF, D_MODEL), 0.5 * D_FF ** -0.5),
    }


def reference(x, rel_bias, norm_mix_g, w_in, b_gate, da_q_norm_g, da_k_norm_g, da_lambda,
              da_subln_g, ret_norm_g, sc_conv_w, sc_conv_b, w_branch_da, w_branch_ret,
              w_branch_sc, w_out, norm_ffn_g, w_ffn_in, ffn_conv_w, ffn_conv_b, w_ffn_out):
    B, S, D = x.shape
    pos = jnp.arange(S)
    bias_dist = rel_bias[rel_bucket(pos)]
    offs = np.cumsum(IN_SPLITS)[:-1].tolist()
    for l in range(DEPTH):
        h = rmsnorm(x, norm_mix_g[l])
        proj = jnp.einsum('bsd,de->bse', h, w_in[l])
        (da_q, da_k, da_v, r_q, r_k, r_v, r_g,
         sc_b, sc_c, sc_x, gate_pre) = jnp.split(proj, offs, axis=-1)

        lam_init = 0.8 - 0.6 * math.exp(-0.3 * l)
        lp = da_lambda[l].astype(jnp.float32)
        lam = jnp.exp(jnp.sum(lp[0] * lp[1])) - jnp.exp(jnp.sum(lp[2] * lp[3])) + lam_init
        q = rmsnorm(da_q.reshape(B, S, DA_HEADS, 2, DA_QK_DIM), da_q_norm_g[l]) * (DA_QK_DIM ** -0.5)
        k = rmsnorm(da_k.reshape(B, S, DA_HEADS, 2, DA_QK_DIM), da_k_norm_g[l])
        v = da_v.reshape(B, S, DA_HEADS, DA_V_DIM)
        o_da = diff_attention(q, k, v, lam, bias_dist)
        o_da = (rmsnorm(o_da, da_subln_g[l]) * (1.0 - lam_init)).astype(x.dtype).reshape(B, S, DA_V_W)

        rq = rotary(r_q.reshape(B, S, RET_HEADS, RET_QK_DIM), pos)
        rk = rotary(r_k.reshape(B, S, RET_HEADS, RET_QK_DIM), pos) * (RET_QK_DIM ** -0.5)
        rv = r_v.reshape(B, S, RET_HEADS, RET_V_DIM)
        o_ret = rmsnorm(retention(rq, rk, rv), ret_norm_g[l]).reshape(B, S, RET_V_W)
        o_ret = (o_ret * jax.nn.silu(r_g.astype(jnp.float32))).astype(x.dtype)

        o_sc = sc_b * causal_dwconv(sc_c * sc_x, sc_conv_w[l], sc_conv_b[l])

        gates = jax.nn.sigmoid((gate_pre + b_gate[l]).astype(jnp.float32)).astype(x.dtype)
        gates = gates.reshape(B, S, N_BRANCH, D)
        y = (gates[:, :, 0] * jnp.einsum('bse,ed->bsd', o_da, w_branch_da[l])
             + gates[:, :, 1] * jnp.einsum('bse,ed->bsd', o_ret, w_branch_ret[l])
             + gates[:, :, 2] * jnp.einsum('bse,ed->bsd', o_sc, w_branch_sc[l]))
        x = x + jnp.einsum('bsd,de->bse', y, w_out[l])

        h = rmsnorm(x, norm_ffn_g[l])
        gu = jnp.einsum('bsd,df->bsf', h, w_ffn_in[l])
        g_ff, u_ff = gu[..., :D_FF], gu[..., D_FF:]
        g_ff = causal_dwconv(g_ff, ffn_conv_w[l], ffn_conv_b[l])
        x = x + jnp.einsum('bsf,fd->bsd', jax.nn.silu(g_ff) * u_ff, w_ffn_out[l])
    return x
```

```python
import math
from contextlib import ExitStack

import numpy as np
import ml_dtypes

import concourse.bass as bass
import concourse.mybir as mybir
from concourse.bass_utils import run_bass_kernel_spmd

F32 = mybir.dt.float32
BF16 = mybir.dt.bfloat16
AF = mybir.ActivationFunctionType
ALU = mybir.AluOpType
AX = mybir.AxisListType

D_MODEL = 1024
DEPTH = 2
SEQ = 4096
BATCH = 8
IN_WIDTH = 6400
D_FF = 2816
NFC = D_FF // 128
EPS = 1e-6
NEG = -30000.0
PRM_L = 587


class Sem:
    __slots__ = ("h", "count")

    def __init__(self, h):
        self.h = h
        self.count = 0


class Res:
    __slots__ = ("name", "last_w", "readers", "dsem", "psum")

    def __init__(self, name, psum=False):
        self.name = name
        self.last_w = None
        self.readers = {}
        self.dsem = None
        self.psum = psum


class Tl:
    def __init__(self, t, res):
        self.t = t
        self.res = res

    def __getitem__(self, k):
        return self.t[k]


class TlView:
    def __init__(self, ap, res):
        self.ap = ap
        self.res = res

    def __getitem__(self, k):
        return self.ap[k]


class Eng:
    def __init__(self, name, e, sem, is_pe=False):
        self.name = name
        self.e = e
        self.sem = sem
        self.waited = {}
        self.is_pe = is_pe


def _res(x):
    return x.res if isinstance(x, (Tl, TlView)) else x


class KB:
    def __init__(self, S, debug=()):
        self.S = S
        self.debug = set(debug)
        self.nc = bass.Bass("TRN2", target_bir_lowering=False)
        self.gstack = ExitStack()
        self.free_sems = []
        self.nsem = 0
        nc = self.nc
        self.pe = Eng("pe", nc.tensor, self.new_sem(), True)
        self.act = Eng("act", nc.scalar, self.new_sem())
        self.dve = Eng("dve", nc.vector, self.new_sem())
        self.pool = Eng("pool", nc.gpsimd, self.new_sem())
        self.sp = Eng("sp", nc.sync, self.new_sem())
        self.engs = [self.pe, self.act, self.dve, self.pool, self.sp]
        self.phase_sems = []
        self.live_dsems = set()
        self.uid = 0

    def new_sem(self):
        if self.free_sems:
            return self.free_sems.pop()
        self.nsem += 1
        h = self.gstack.enter_context(self.nc.semaphore(f"s{self.nsem}"))
        return Sem(h)

    def tile(self, es, name, shape, dt, psum=False):
        self.uid += 1
        nm = f"{name}_{self.uid}"
        if psum:
            t = es.enter_context(self.nc.psum_tensor(nm, shape, dt))
        else:
            t = es.enter_context(self.nc.sbuf_tensor(nm, shape, dt))
        return Tl(t, Res(nm, psum))

    def psum_bf16_views(self, es, name, n):
        out = []
        for i in range(n):
            if i % 2 == 0:
                bank = self.tile(es, f"{name}b{i // 2}", [128, 512], F32, True)
                v = bank[:].bitcast(BF16)
            out.append(TlView(v[:, (i % 2) * 512:(i % 2 + 1) * 512], bank.res))
        return out

    def dram(self, name, shape, dt):
        kind = "ExternalOutput" if name in self.debug else "Internal"
        return self.nc.dram_tensor(name, shape, dt, kind=kind).ap()

    def _waits(self, E, reads, writes):
        need = {}

        def add(t):
            if t is None:
                return
            sem, val = t
            if E.is_pe and sem is E.sem:
                return
            if need.get(sem, 0) < val:
                need[sem] = val

        for r in reads:
            r = _res(r)
            add(r.last_w)
            if r.psum:
                for sem, val in r.readers.items():
                    if sem is not E.sem:
                        add((sem, val))
        for w in writes:
            w = _res(w)
            add(w.last_w)
            for sem, val in w.readers.items():
                add((sem, val))
        for sem, val in need.items():
            if E.waited.get(sem, 0) >= val:
                continue
            E.e.wait_ge(sem.h, val)
            E.waited[sem] = val

    def _commit(self, tok, reads, writes):
        sem, val = tok
        for r in reads:
            r = _res(r)
            if r.readers.get(sem, 0) < val:
                r.readers[sem] = val
        for w in writes:
            w = _res(w)
            w.last_w = tok
            w.readers = {}

    def op(self, E, fn, reads=(), writes=()):
        self._waits(E, reads, writes)
        ins = fn()
        E.sem.count += 1
        ins.then_inc(E.sem.h, 1)
        self._commit((E.sem, E.sem.count), reads, writes)
        return ins

    def dma(self, E, out, in_, reads=(), writes=(), sem=None, big=False, **kw):
        if E is self.sp and not big:
            E = self.pool
        if E is self.sp:
            n = 1
            apl = list((out if out.tensor.__class__.__name__.startswith("DRam") else in_).ap)
            for st, cnt in apl[:-1]:
                n *= cnt
            self.hw_desc = getattr(self, "hw_desc", 0) + max(n, 128)
        if sem is None:
            owner = None
            for x in list(writes) + list(reads):
                if isinstance(x, Tl):
                    owner = x.res
                    break
            assert owner is not None
            if owner.dsem is None:
                owner.dsem = self.new_sem()
                self.phase_sems.append(owner.dsem)
            sem = owner.dsem
        self._waits(E, reads, writes)
        ins = E.e.dma_start(out=out, in_=in_, **kw)
        sem.count += 16
        ins.then_inc(sem.h, 16)
        self.live_dsems.add(sem)
        self._commit((sem, sem.count), reads, writes)
        return ins

    def barrier(self):
        sp = self.sp
        for sem in list(self.live_dsems):
            if sp.waited.get(sem, 0) < sem.count:
                sp.e.wait_ge(sem.h, sem.count)
                sp.waited[sem] = sem.count
        self.live_dsems = set()
        for F in self.engs:
            if F is sp:
                continue
            if F.sem.count > sp.waited.get(F.sem, 0):
                sp.e.wait_ge(F.sem.h, F.sem.count)
                sp.waited[F.sem] = F.sem.count
        ins = sp.e.nop()
        sp.sem.count += 1
        ins.then_inc(sp.sem.h, 1)
        for F in self.engs:
            if F is sp:
                continue
            F.e.wait_ge(sp.sem.h, sp.sem.count)
            F.waited[sp.sem] = sp.sem.count

    def end_phase(self):
        self.barrier()
        self.free_sems.extend(self.phase_sems)
        self.phase_sems = []


class Rot:
    def __init__(self, tiles):
        self.tiles = tiles
        self.i = 0

    def next(self):
        t = self.tiles[self.i % len(self.tiles)]
        self.i += 1
        return t


def bc(ap, axis, n):
    a = ap.unsqueeze(axis)
    shp = list(a.shape)
    shp[axis] = n
    return a.broadcast_to(shp)


class Prog:
    def __init__(self, S, debug=(), upto=None, nlayers=DEPTH):
        self.S = S
        self.NB = S // 512
        self.NT = S // 128
        self.upto = upto
        self.nlayers = nlayers
        kb = self.kb = KB(S, debug)
        nc = self.nc = kb.nc
        NT = self.NT
        ext = lambda name, shape, dt=F32: nc.dram_tensor(name, shape, dt, kind="ExternalInput").ap()
        self.x = ext("x", [S, D_MODEL])
        self.w_in = ext("w_in", [DEPTH, D_MODEL, IN_WIDTH])
        self.w_bd = ext("w_branch_da", [DEPTH, 512, D_MODEL])
        self.w_br = ext("w_branch_ret", [DEPTH, 256, D_MODEL])
        self.w_bs = ext("w_branch_sc", [DEPTH, 256, D_MODEL])
        self.w_o = ext("w_out", [DEPTH, D_MODEL, D_MODEL])
        self.w_fi = ext("w_ffn_in", [DEPTH, D_MODEL, 2 * D_FF])
        self.w_fo = ext("w_ffn_out", [DEPTH, D_FF, D_MODEL])
        self.NC = 2 * NT * 32 + 512 + 256 + 4 + 2
        self.cst_in = ext("cst", [128, self.NC])
        self.prm_in = ext("prm", [128, DEPTH * PRM_L + 4])
        self.cbf_in = ext("cbf", [128, 256], BF16)
        self.G_in = ext("gbias", [128, 4, 1024])
        self.out = nc.dram_tensor("out", [S, D_MODEL], F32, kind="ExternalOutput").ap()
        d = kb.dram
        self.wb = {}
        for l in range(DEPTH):
            self.wb[("in", l)] = d(f"wb_in{l}", [D_MODEL, IN_WIDTH], BF16)
            self.wb[("bd", l)] = d(f"wb_bd{l}", [512, D_MODEL], BF16)
            self.wb[("br", l)] = d(f"wb_br{l}", [256, D_MODEL], BF16)
            self.wb[("bs", l)] = d(f"wb_bs{l}", [256, D_MODEL], BF16)
            self.wb[("o", l)] = d(f"wb_o{l}", [D_MODEL, D_MODEL], BF16)
            self.wb[("fi", l)] = d(f"wb_fi{l}", [D_MODEL, 2 * D_FF], BF16)
            self.wb[("fo", l)] = d(f"wb_fo{l}", [D_FF, D_MODEL], BF16)
        self.wres = {k: Res(f"w{k}") for k in self.wb}
        self.qT = d("s_qT", [4, 128, S], BF16)
        self.kT = d("s_kT", [4, 128, S], BF16)
        self.vS = d("s_v", [S, 512], BF16)
        self.rqT = d("s_rqT", [2, 128, S], BF16)
        self.rkT = d("s_rkT", [2, 128, S], BF16)
        self.rkd = d("s_rkd", [S, 256], BF16)
        self.rv = d("s_rv", [S, 256], BF16)
        self.rgg = d("s_rgg", [S, 256], F32)
        self.sc = d("s_sc", [3, 2, 128, S], F32)
        self.gates = d("s_gates", [24, 128, S], BF16)
        self.odaT = d("s_odaT", [4, 128, S], BF16)
        self.oretT = d("s_oretT", [2, 128, S], BF16)
        self.oscT = d("s_oscT", [2, 128, S], BF16)
        self.x1 = d("s_x1", [S, D_MODEL], F32)
        self.x2 = d("s_x2", [S, D_MODEL], F32)

    def build(self):
        kb = self.kb
        self.cast_q = []
        self.cast_enqueue(0)
        self.cast_enqueue(1)
        xin = self.x
        order = ["A", "B1", "B2", "C", "D"]
        stop = False
        for l in range(self.nlayers):
            last = l == self.nlayers - 1
            xout = self.out if last else self.x2
            import os
            skip = os.environ.get("PH_SKIP", "").split(",")
            for ph in order:
                if ph in skip:
                    continue
                if ph == "A":
                    self.phase_A(l, xin)
                elif ph == "B1":
                    self.phase_B1(l)
                elif ph == "B2":
                    self.phase_B2(l)
                elif ph == "C":
                    self.phase_C(l, xin)
                elif ph == "D":
                    self.phase_D(l, xout)
                if self.upto == (l, ph):
                    stop = True
                    break
            if stop:
                break
            xin = self.x2
        kb.barrier()
        kb.gstack.close()
        return self.nc

    def cast_enqueue(self, l):
        kb = self.kb
        if l >= self.nlayers:
            return
        srcs = {"in": self.w_in, "bd": self.w_bd, "br": self.w_br, "bs": self.w_bs, "o": self.w_o,
                "fi": self.w_fi, "fo": self.w_fo}
        for key in ["in", "bd", "br", "bs", "o", "fi", "fo"]:
            src = srcs[key][l]
            dst = self.wb[(key, l)]
            rows, cols = dst.shape
            res = self.wres[(key, l)]
            sem = kb.new_sem()
            ncs = (cols + 3199) // 3200
            cw = cols // ncs
            jobs = []
            for r0 in range(0, rows, 128):
                for ci in range(ncs):
                    jobs.append((r0, ci))
            for n, (r0, ci) in enumerate(jobs):
                def fn(dst=dst, src=src, r0=r0, ci=ci, cw=cw, sem=sem, res=res, last=(n == len(jobs) - 1)):
                    kb.dma(kb.pool, dst[r0:r0 + 128, ci * cw:(ci + 1) * cw], src[r0:r0 + 128, ci * cw:(ci + 1) * cw],
                           reads=(), writes=(), sem=sem, max_dma_last_dim=4096)
                    if last:
                        res.last_w = (sem, sem.count)
                self.cast_q.append(((key, l), fn))

    def cast_issue(self, n):
        for _ in range(min(n, len(self.cast_q))):
            self.cast_q.pop(0)[1]()

    def cast_need(self, *keys):
        while any(k in keys for k, _ in self.cast_q):
            self.cast_q.pop(0)[1]()

    def load_common(self, es, l, need_cst=False):
        kb = self.kb
        prm = kb.tile(es, "prm", [128, DEPTH * PRM_L + 4], F32)
        kb.dma(kb.sp, prm[:], self.prm_in, writes=[prm], big=True)
        cbf = kb.tile(es, "cbf", [128, 256], BF16)
        kb.dma(kb.sp, cbf[:], self.cbf_in, writes=[cbf], big=True)
        cst = None
        if need_cst:
            cst = kb.tile(es, "cst", [128, self.NC], F32)
            kb.dma(kb.sp, cst[:], self.cst_in, writes=[cst], big=True)
        return prm, cbf, cst

    def pcol(self, prm, l, off, n=1):
        o = l * PRM_L + off
        return prm[:, o:o + n]

    P_NG, P_FG, P_BG, P_QG, P_KG, P_SCW, P_SCB, P_FCW, P_FCB, P_LAM, P_SUBG, P_RNG, P_SUBGC = (
        0, 8, 16, 40, 41, 42, 48, 50, 116, 138, 394, 522, 586)

    def rmsnorm_T(self, l, goff, xt, xb, hT, trp, prm, cbf, small, E_evac=None):
        self.rmsnorm_act(xt, xb, small)
        self.rmsnorm_tr(l, goff, xb, hT, trp, prm, cbf)

    def rmsnorm_act(self, xt, xb, small):
        kb = self.kb
        nc = self.nc
        ssq, rstd = small["ssq"], small["rstd"]
        for s in range(4):
            kb.op(kb.act, lambda s=s: nc.scalar.activation(out=xb[:, s, :], in_=xt[:, s, :], func=AF.Square,
                                                           accum_out=ssq[:, s:s + 1]),
                  reads=[xt], writes=[xb, ssq])
        kb.op(kb.act, lambda: nc.scalar.activation(out=rstd[:], in_=ssq[:], func=AF.Ln, bias=small["eps"][:, 0:1],
                                                   scale=1.0 / D_MODEL),
              reads=[ssq, small["eps"]], writes=[rstd])
        kb.op(kb.act, lambda: nc.scalar.activation(out=rstd[:], in_=rstd[:], func=AF.Exp, scale=-0.5),
              reads=[rstd], writes=[rstd])
        for s in range(4):
            kb.op(kb.act, lambda s=s: nc.scalar.activation(out=xb[:, s, :], in_=xt[:, s, :], func=AF.Identity,
                                                           scale=rstd[:, s:s + 1]),
                  reads=[xt, rstd], writes=[xb])

    def rmsnorm_tr(self, l, goff, xb, hT, trp, prm, cbf):
        kb = self.kb
        nc = self.nc
        ident = cbf[:, 0:128]
        for half in range(2):
            for i in range(4):
                c = half * 4 + i
                tp = trp[i]
                for s in range(4):
                    kb.op(kb.pe, lambda s=s, c=c, tp=tp: nc.tensor.transpose(
                        out=tp[:, s * 128:(s + 1) * 128], in_=xb[:, s, c * 128:(c + 1) * 128], identity=ident),
                          reads=[xb, cbf], writes=[tp])
            for i in range(4):
                c = half * 4 + i
                tp = trp[i]
                kb.op(kb.dve, lambda c=c, tp=tp: nc.vector.tensor_scalar(
                    out=hT[:, c, :], in0=tp[:], scalar1=self.pcol(prm, l, goff + c), scalar2=None, op0=ALU.mult),
                      reads=[tp, prm], writes=[hT])

    def phase_A(self, l, xsrc):
        kb, nc, S, NB, NT = self.kb, self.nc, self.S, self.NB, self.NT
        P = self
        with ExitStack() as es:
            T = lambda name, shape, dt=F32, psum=False: kb.tile(es, name, shape, dt, psum)
            prm, cbf, cst = self.load_common(es, l, need_cst=True)
            self.cast_need(("in", l))
            w = T("wA", [128, 8, IN_WIDTH], BF16)
            wsrc = self.wb[("in", l)]
            for kc in range(8):
                kb.dma(kb.sp, w[:, kc, :], wsrc[kc * 128:(kc + 1) * 128, :], reads=[self.wres[("in", l)]], writes=[w], big=True)
            xt = T("xt", [128, 4, D_MODEL])
            xb = T("xb", [128, 4, D_MODEL], BF16)
            hT = T("hT", [128, 8, 512], BF16)
            small = {"ssq": T("ssq", [128, 4]), "rstd": T("rstd", [128, 4]), "eps": T("eps", [128, 4])}
            kb.op(kb.dve, lambda: nc.vector.memset(small["eps"][:, 0:1], EPS), writes=[small["eps"]])
            kb.op(kb.dve, lambda: nc.vector.memset(small["eps"][:, 1:2], 64 * EPS), writes=[small["eps"]])
            mm = Rot([T(f"mm{i}", [128, 512], F32, True) for i in range(3)])
            ssqp = T("ssqp", [128, 512], F32, True)
            trp = kb.psum_bf16_views(es, "trp", 8)
            vst = Rot([T(f"vst{i}", [128, 512], BF16) for i in range(2)])
            tt = [Rot([T(f"t{k}_{i}", [128, 8, 32]) for i in range(2)]) for k in range(4)]
            rot = Rot([T(f"rot{i}", [128, 8, 2, 32]) for i in range(2)])
            rqb = Rot([T(f"rqb{i}", [128, 256], BF16) for i in range(2)])
            rkb = Rot([T(f"rkb{i}", [128, 256], BF16) for i in range(2)])
            rkdt = Rot([T(f"rkd{i}", [128, 4, 64], BF16) for i in range(2)])
            rvb = Rot([T(f"rvb{i}", [128, 256], BF16) for i in range(2)])
            sg = Rot([T(f"sg{i}", [128, 256]) for i in range(2)])
            sg2 = Rot([T(f"sg2{i}", [128, 4, 64]) for i in range(2)])
            rggt = Rot([T(f"rgg{i}", [128, 4, 64]) for i in range(2)])
            trst = Rot([T(f"trst{i}", [128, 512], BF16) for i in range(8)])
            sqt = Rot([T(f"sq{i}", [128, 512], BF16) for i in range(2)])
            rst = Rot([T(f"rs{i}", [128, 512]) for i in range(2)])
            qnt = Rot([T(f"qn{i}", [128, 512], BF16) for i in range(2)])
            scst = Rot([T(f"scst{i}", [128, 2, 512]) for i in range(2)])
            gst = Rot([T(f"gst{i}", [128, 2, 512], BF16) for i in range(3)])
            cosv = cst[:, 0:NT * 32].rearrange("p (t d) -> p t d", d=32)
            sinv = cst[:, NT * 32:2 * NT * 32].rearrange("p (t d) -> p t d", d=32)
            dk8 = cst[:, 2 * NT * 32 + 768:2 * NT * 32 + 772]
            rng = self.pcol(prm, l, self.P_RNG, 64)
            blk1 = cbf[:, 128:256]

            def load_x(b):
                kb.dma(kb.sp, xt[:], xsrc[b * 512:(b + 1) * 512, :].rearrange("(s p) d -> p s d", p=128), writes=[xt])

            load_x(0)
            self.rmsnorm_act(xt, xb, small)
            for b in range(NB):
                t0 = b * 512
                self.cast_issue(7)
                self.rmsnorm_tr(l, self.P_NG, xb, hT, trp[0:4], prm, cbf)
                if b + 1 < NB:
                    load_x(b + 1)
                trq = [trp[4], trp[5], trp[6], trp[7]]
                tr_pend = []
                for s in range(4):
                    tix = b * 4 + s
                    r0 = t0 + s * 128
                    for grp in range(3):
                        ps = mm.next()
                        c0 = 1024 + grp * 512
                        for kc in range(8):
                            kb.op(kb.pe, lambda kc=kc, ps=ps, c0=c0, s=s: nc.tensor.matmul(
                                ps[:], lhsT=hT[:, kc, s * 128:(s + 1) * 128], rhs=w[:, kc, c0:c0 + 512],
                                start=(kc == 0), stop=(kc == 7)), reads=[hT, w], writes=[ps])
                        if grp == 2:
                            while len(tr_pend) > 1:
                                tr_pend.pop(0)()
                        if grp == 0:
                            o = vst.next()
                            kb.op(kb.act, lambda o=o, ps=ps: nc.scalar.copy(out=o[:], in_=ps[:]), reads=[ps], writes=[o])
                            kb.dma(kb.sp, self.vS[r0:r0 + 128, :], o[:], reads=[o])
                        elif grp == 1:
                            pv = ps[:].rearrange("p (h two d) -> p h two d", two=2, d=32)
                            x1, x2 = pv[:, :, 0, :], pv[:, :, 1, :]
                            cb = bc(cosv[:, tix, :], 1, 8)
                            sb = bc(sinv[:, tix, :], 1, 8)
                            t1, t2, t3, t4 = [r.next() for r in tt]
                            ro = rot.next()
                            TT = nc.vector.tensor_tensor
                            kb.op(kb.dve, lambda: TT(out=t1[:], in0=x1, in1=cb, op=ALU.mult), reads=[ps, cst], writes=[t1])
                            kb.op(kb.dve, lambda: TT(out=t2[:], in0=x2, in1=sb, op=ALU.mult), reads=[ps, cst], writes=[t2])
                            kb.op(kb.dve, lambda: TT(out=t3[:], in0=x2, in1=cb, op=ALU.mult), reads=[ps, cst], writes=[t3])
                            kb.op(kb.dve, lambda: TT(out=t4[:], in0=x1, in1=sb, op=ALU.mult), reads=[ps, cst], writes=[t4])
                            kb.op(kb.pool, lambda: nc.gpsimd.tensor_tensor(out=ro[:, :, 0, :], in0=t1[:], in1=t2[:], op=ALU.subtract),
                                  reads=[t1, t2], writes=[ro])
                            kb.op(kb.pool, lambda: nc.gpsimd.tensor_tensor(out=ro[:, :, 1, :], in0=t3[:], in1=t4[:], op=ALU.add),
                                  reads=[t3, t4], writes=[ro])
                            rof = ro[:].rearrange("p h two d -> p (h two d)")
                            qb, kbb, kd = rqb.next(), rkb.next(), rkdt.next()
                            kb.op(kb.act, lambda: nc.scalar.copy(out=qb[:], in_=rof[:, 0:256]), reads=[ro], writes=[qb])
                            kb.op(kb.act, lambda: nc.scalar.mul(out=kbb[:], in_=rof[:, 256:512], mul=0.125), reads=[ro], writes=[kbb])
                            kb.op(kb.dve, lambda: TT(out=kd[:], in0=rof[:, 256:512].rearrange("p (h d) -> p h d", d=64),
                                                     in1=bc(dk8, 2, 64), op=ALU.mult), reads=[ro, cst], writes=[kd])
                            kb.dma(kb.sp, self.rkd[r0:r0 + 128, :], kd[:].rearrange("p h d -> p (h d)"), reads=[kd])
                            def do_tr(s=s, qb=qb, kbb=kbb):
                                ident = cbf[:, 0:128]
                                for cc in range(2):
                                    kb.op(kb.pe, lambda cc=cc: nc.tensor.transpose(
                                        out=trq[cc][:, s * 128:(s + 1) * 128], in_=qb[:, cc * 128:(cc + 1) * 128], identity=ident),
                                          reads=[qb, cbf], writes=[trq[cc]])
                                    kb.op(kb.pe, lambda cc=cc: nc.tensor.transpose(
                                        out=trq[2 + cc][:, s * 128:(s + 1) * 128], in_=kbb[:, cc * 128:(cc + 1) * 128], identity=ident),
                                          reads=[kbb, cbf], writes=[trq[2 + cc]])
                            tr_pend.append(do_tr)
                        else:
                            o = rvb.next()
                            kb.op(kb.act, lambda o=o, ps=ps: nc.scalar.copy(out=o[:], in_=ps[:, 0:256]), reads=[ps], writes=[o])
                            kb.dma(kb.sp, self.rv[r0:r0 + 128, :], o[:], reads=[o])
                            g1, g2, g3 = sg.next(), sg2.next(), rggt.next()
                            kb.op(kb.act, lambda g1=g1, ps=ps: nc.scalar.activation(out=g1[:], in_=ps[:, 256:512], func=AF.Sigmoid),
                                  reads=[ps], writes=[g1])
                            kb.op(kb.dve, lambda g1=g1, g2=g2, ps=ps: nc.vector.tensor_tensor(
                                out=g2[:].rearrange("p h d -> p (h d)"), in0=ps[:, 256:512], in1=g1[:], op=ALU.mult),
                                  reads=[ps, g1], writes=[g2])
                            kb.op(kb.pool, lambda g2=g2, g3=g3: nc.gpsimd.tensor_tensor(
                                out=g3[:], in0=g2[:], in1=bc(rng, 1, 4), op=ALU.mult), reads=[g2, prm], writes=[g3])
                            kb.dma(kb.sp, self.rgg[r0:r0 + 128, :], g3[:].rearrange("p h d -> p (h d)"), reads=[g3])
                if b + 1 < NB:
                    self.rmsnorm_act(xt, xb, small)
                while tr_pend:
                    tr_pend.pop(0)()
                dsts = [self.rqT[0], self.rqT[1], self.rkT[0], self.rkT[1]]
                for i in range(4):
                    o = trst.next()
                    kb.op(kb.dve, lambda o=o, i=i: nc.vector.tensor_copy(out=o[:], in_=trq[i][:]), reads=[trq[i]], writes=[o])
                    kb.dma(kb.sp, dsts[i][:, t0:t0 + 512], o[:], reads=[o])
                pend = []

                def qk_post(ps, ci):
                    isq = ci < 4
                    h = ci % 4
                    sq = sqt.next()
                    kb.op(kb.act, lambda: nc.scalar.activation(out=sq[:], in_=ps[:], func=AF.Square), reads=[ps], writes=[sq])

                    def second():
                        kb.op(kb.pe, lambda: nc.tensor.matmul(ssqp[:], lhsT=blk1, rhs=sq[:], start=True, stop=True),
                              reads=[sq, cbf], writes=[ssqp])
                        rs = rst.next()
                        kb.op(kb.act, lambda: nc.scalar.activation(
                            out=rs[:], in_=ssqp[:], func=AF.Ln, bias=small["eps"][:, 1:2] if isq else small["eps"][:, 0:1],
                            scale=1.0 if isq else 1.0 / 64), reads=[ssqp, small["eps"]], writes=[rs])
                        kb.op(kb.act, lambda: nc.scalar.activation(out=rs[:], in_=rs[:], func=AF.Exp, scale=-0.5),
                              reads=[rs], writes=[rs])
                        qn = qnt.next()
                        gcol = self.pcol(prm, l, self.P_QG if isq else self.P_KG)
                        kb.op(kb.dve, lambda: nc.vector.scalar_tensor_tensor(
                            out=qn[:], in0=ps[:], scalar=gcol, in1=rs[:], op0=ALU.mult, op1=ALU.mult),
                              reads=[ps, rs, prm], writes=[qn])
                        dst = self.qT if isq else self.kT
                        kb.dma(kb.sp, dst[h][:, t0:t0 + 512], qn[:], reads=[qn])
                    return second

                qkc = [("qk", ci, ci * 128) for ci in range(8)]
                scc = [("sc", ci, 2560 + ci * 128) for ci in range(6)]
                chunks = []
                for ci in range(8):
                    chunks.append(qkc[ci])
                    if ci < 6:
                        chunks.append(scc[ci])
                chunks += [("g", ci, 3328 + ci * 128) for ci in range(24)]
                for kind, ci, c0 in chunks:
                    ps = mm.next()
                    for kc in range(8):
                        kb.op(kb.pe, lambda kc=kc, ps=ps, c0=c0: nc.tensor.matmul(
                            ps[:], lhsT=w[:, kc, c0:c0 + 128], rhs=hT[:, kc, :], start=(kc == 0), stop=(kc == 7)),
                              reads=[hT, w], writes=[ps])
                    for f in pend:
                        f()
                    pend = []
                    if kind == "qk":
                        pend.append(qk_post(ps, ci))
                    elif kind == "sc":
                        if ci % 2 == 0:
                            sc_o = scst.next()
                        o = sc_o
                        kb.op(kb.act, lambda o=o, ps=ps, ci=ci: nc.scalar.copy(out=o[:, ci % 2, :], in_=ps[:]), reads=[ps], writes=[o])
                        if ci % 2 == 1:
                            kb.dma(kb.sp, self.sc[ci // 2][:, :, t0:t0 + 512].rearrange("a p t -> p a t"), o[:], reads=[o])
                    else:
                        if ci % 2 == 0:
                            g_o = gst.next()
                        o = g_o
                        kb.op(kb.act, lambda o=o, ps=ps, ci=ci: nc.scalar.activation(
                            out=o[:, ci % 2, :], in_=ps[:], func=AF.Sigmoid, bias=self.pcol(prm, l, self.P_BG + ci)),
                              reads=[ps, prm], writes=[o])
                        if ci % 2 == 1:
                            kb.dma(kb.sp, self.gates[ci - 1:ci + 1, :, t0:t0 + 512].rearrange("a p t -> p a t"), o[:], reads=[o])
                for f in pend:
                    f()
            kb.end_phase()

    def phase_B1(self, l):
        kb, nc, S, NB, NT = self.kb, self.nc, self.S, self.NB, self.NT
        lam_init = 0.8 - 0.6 * math.exp(-0.3 * l)
        with ExitStack() as es:
            T = lambda name, shape, dt=F32, psum=False: kb.tile(es, name, shape, dt, psum)
            prm, cbf, _ = self.load_common(es, l)
            G = T("G", [128, 4, 1024])
            kb.dma(kb.sp, G[:], self.G_in, writes=[G], big=True)
            lamv = self.pcol(prm, l, self.P_LAM, 256)
            ltmp = T("ltmp", [128, 64])
            s12 = T("s12", [128, 2])
            e12 = T("e12", [128, 2])
            nlam = T("nlam", [128, 1])
            epsb = T("epsb", [128, 2])
            kb.op(kb.dve, lambda: nc.vector.memset(epsb[:, 0:1], EPS), writes=[epsb])
            kb.op(kb.dve, lambda: nc.vector.memset(epsb[:, 1:2], math.log(1.0 - lam_init)), writes=[epsb])
            for k in range(2):
                kb.op(kb.dve, lambda k=k: nc.vector.tensor_tensor(out=ltmp[:], in0=lamv[:, 128 * k:128 * k + 64],
                                                                 in1=lamv[:, 128 * k + 64:128 * k + 128], op=ALU.mult),
                      reads=[prm], writes=[ltmp])
                kb.op(kb.dve, lambda k=k: nc.vector.tensor_reduce(out=s12[:, k:k + 1], in_=ltmp[:], axis=AX.X, op=ALU.add),
                      reads=[ltmp], writes=[s12])
            kb.op(kb.act, lambda: nc.scalar.activation(out=e12[:], in_=s12[:], func=AF.Exp), reads=[s12], writes=[e12])
            kb.op(kb.dve, lambda: nc.vector.tensor_tensor(out=nlam[:], in0=e12[:, 1:2], in1=e12[:, 0:1], op=ALU.subtract),
                  reads=[e12], writes=[nlam])
            kb.op(kb.dve, lambda: nc.vector.tensor_scalar(out=nlam[:], in0=nlam[:], scalar1=-lam_init, scalar2=None, op0=ALU.add),
                  reads=[nlam], writes=[nlam])
            tb = T("sc_b", [128, S])
            tc = T("sc_c", [128, S])
            tx = T("sc_x", [128, S])
            ty = T("sc_y", [128, S])
            tob = T("sc_o", [128, S], BF16)
            GP = nc.gpsimd
            conv_q = []
            for cc in range(2):
                wc = lambda k, cc=cc: self.pcol(prm, l, self.P_SCW + cc * 3 + k)
                bcol = self.pcol(prm, l, self.P_SCB + cc)

                def c_load(cc=cc):
                    kb.dma(kb.sp, tb[:], self.sc[0, cc], writes=[tb], big=True)
                    kb.dma(kb.sp, tc[:], self.sc[1, cc], writes=[tc], big=True)
                    kb.dma(kb.sp, tx[:], self.sc[2, cc], writes=[tx], big=True)
                conv_q.append(c_load)
                conv_q.append(lambda: kb.op(kb.pool, lambda: GP.tensor_tensor(out=tc[:], in0=tc[:], in1=tx[:], op=ALU.mult),
                                            reads=[tc, tx], writes=[tc]))
                conv_q.append(lambda wc=wc, bcol=bcol: kb.op(kb.pool, lambda: GP.tensor_scalar(
                    out=ty[:], in0=tc[:], scalar1=wc(2), scalar2=bcol, op0=ALU.mult, op1=ALU.add), reads=[tc, prm], writes=[ty]))
                conv_q.append(lambda wc=wc: kb.op(kb.pool, lambda: GP.tensor_scalar(
                    out=tx[:], in0=tc[:], scalar1=wc(1), scalar2=0.0, op0=ALU.mult, op1=ALU.add), reads=[tc, prm], writes=[tx]))
                conv_q.append(lambda: kb.op(kb.pool, lambda: GP.tensor_tensor(out=ty[:, 1:S], in0=ty[:, 1:S], in1=tx[:, 0:S - 1], op=ALU.add),
                                            reads=[ty, tx], writes=[ty]))
                conv_q.append(lambda wc=wc: kb.op(kb.pool, lambda: GP.tensor_scalar(
                    out=tx[:], in0=tc[:], scalar1=wc(0), scalar2=0.0, op0=ALU.mult, op1=ALU.add), reads=[tc, prm], writes=[tx]))
                conv_q.append(lambda: kb.op(kb.pool, lambda: GP.tensor_tensor(out=ty[:, 2:S], in0=ty[:, 2:S], in1=tx[:, 0:S - 2], op=ALU.add),
                                            reads=[ty, tx], writes=[ty]))
                conv_q.append(lambda: kb.op(kb.pool, lambda: GP.tensor_tensor(out=tob[:], in0=tb[:], in1=ty[:], op=ALU.mult),
                                            reads=[tb, ty], writes=[tob]))
                conv_q.append(lambda cc=cc: kb.dma(kb.sp, self.oscT[cc], tob[:], reads=[tob], big=True))
            qTs = Rot([T(f"qT{i}", [128, S], BF16) for i in range(2)])
            kTs = Rot([T(f"kT{i}", [128, S], BF16) for i in range(2)])
            Vs = Rot([T(f"V{i}", [128, NT, 128], BF16) for i in range(2)])
            ones_bf = T("ones_bf", [128, 128], BF16)
            kb.op(kb.dve, lambda: nc.vector.memset(ones_bf[:], 1.0), writes=[ones_bf])
            STb = Rot([T(f"st{a}", [128, 512], F32, True) for a in range(3)])
            accT = [T(f"accT{m}", [128, 512], F32, True) for m in range(2)]
            lsum = [T(f"lsum{m}", [128, 512], F32, True) for m in range(2)]
            ssqb = T("ssqb", [128, 512], F32, True)
            PTs = Rot([T(f"pt{i}", [128, 512], BF16) for i in range(6)])
            sbs = Rot([T(f"sb{i}", [128, 512]) for i in range(3)])
            rls = [Rot([T(f"rl{m}{i}", [128, 512]) for i in range(2)]) for m in range(2)]
            tts = Rot([T(f"et{i}", [128, 512]) for i in range(2)])
            uus = Rot([T(f"eu{i}", [128, 512]) for i in range(2)])
            sqs = Rot([T(f"esq{i}", [128, 512], BF16) for i in range(2)])
            rss = Rot([T(f"ers{i}", [128, 512]) for i in range(2)])
            osts = Rot([T(f"ost{i}", [128, 512], BF16) for i in range(2)])
            subgc = self.pcol(prm, l, self.P_SUBGC)
            cb0 = DEPTH * PRM_L

            def load_head(h):
                q, k, v = qTs.next(), kTs.next(), Vs.next()
                kb.dma(kb.sp, q[:], self.qT[h], writes=[q], big=True)
                kb.dma(kb.sp, k[:], self.kT[h], writes=[k], big=True)
                for t8 in range(0, NT, 4):
                    kb.dma(kb.sp, v[:, t8:t8 + 4, :],
                           self.vS[t8 * 128:(t8 + 4) * 128, h * 128:(h + 1) * 128].rearrange("(t p) e -> p t e", p=128), writes=[v])
                return q, k, v

            heads = {0: load_head(0)}
            items = []
            for h in range(4):
                for i in range(NB):
                    nj = 4 * i + 4
                    for j in range(nj):
                        items.append((h, i, j, nj))
            state = {}
            deferred = []

            def qk_exp(it, idx):
                h, i, j, nj = it
                q, k, v = heads[h]
                delta = 512 * i - 128 * j
                c0 = max(0, -delta)
                pts = []
                sts = []
                for m in range(2):
                    st = STb.next()
                    kb.op(kb.pe, lambda m=m, st=st: nc.tensor.matmul(
                        st[:, c0:512], lhsT=k[m * 64:(m + 1) * 64, j * 128:(j + 1) * 128],
                        rhs=q[m * 64:(m + 1) * 64, i * 512 + c0:(i + 1) * 512], start=True, stop=True),
                          reads=[q, k], writes=[st])
                    sts.append(st)
                for m in range(2):
                    st = sts[m]
                    pt = PTs.next()
                    if delta <= 128:
                        sb = sbs.next()
                        kb.op(kb.dve, lambda: nc.vector.tensor_tensor(
                            out=sb[:, c0:512], in0=st[:, c0:512], in1=G[:, h, delta + 384 + c0:delta + 896], op=ALU.add),
                              reads=[st, G], writes=[sb])
                        kb.op(kb.act, lambda: nc.scalar.activation(out=pt[:, c0:512], in_=sb[:, c0:512], func=AF.Exp),
                              reads=[sb], writes=[pt])
                    else:
                        kb.op(kb.act, lambda: nc.scalar.activation(
                            out=pt[:], in_=st[:], func=AF.Exp, bias=prm[:, cb0 + h:cb0 + h + 1]),
                              reads=[st, prm], writes=[pt])
                    pts.append(pt)
                state[idx] = pts
                if idx % 8 == 4:
                    self.cast_issue(1)
                if idx % 3 == 1 and conv_q:
                    conv_q.pop(0)()

            def pv(it, idx):
                h, i, j, nj = it
                if i == 0 and j == 0 and h + 1 < 4:
                    heads[h + 1] = load_head(h + 1)
                q, k, v = heads[h]
                pts = state.pop(idx)
                delta = 512 * i - 128 * j
                c0 = max(0, -delta)
                for m in range(2):
                    kb.op(kb.pe, lambda m=m: nc.tensor.matmul(
                        accT[m][:, c0:512], lhsT=v[:, j, :], rhs=pts[m][:, c0:512], start=(j == 0), stop=(j == nj - 1)),
                          reads=[pts[m], v], writes=[accT[m]])
                for m in range(2):
                    kb.op(kb.pe, lambda m=m: nc.tensor.matmul(
                        lsum[m][:, c0:512], lhsT=ones_bf[:], rhs=pts[m][:, c0:512], start=(j == 0), stop=(j == nj - 1)),
                          reads=[pts[m], ones_bf], writes=[lsum[m]])
                if j == nj - 1:
                    epilogue(h, i, idx)

            def epilogue(h, i, idx):
                rl = [rls[m].next() for m in range(2)]
                for m in range(2):
                    kb.op(kb.act, lambda m=m: nc.scalar.activation(out=rl[m][:], in_=lsum[m][:], func=AF.Ln), reads=[lsum[m]], writes=[rl[m]])
                for m in range(2):
                    kb.op(kb.act, lambda m=m: nc.scalar.activation(out=rl[m][:], in_=rl[m][:], func=AF.Exp, scale=-1.0), reads=[rl[m]], writes=[rl[m]])
                t = tts.next()
                u = uus.next()
                kb.op(kb.dve, lambda: nc.vector.scalar_tensor_tensor(out=t[:], in0=accT[1][:], scalar=nlam[:, 0:1], in1=rl[1][:],
                                                                    op0=ALU.mult, op1=ALU.mult), reads=[accT[1], nlam, rl[1]], writes=[t])
                kb.op(kb.dve, lambda: nc.vector.tensor_tensor(out=u[:], in0=accT[0][:], in1=rl[0][:], op=ALU.mult),
                      reads=[accT[0], rl[0]], writes=[u])
                kb.op(kb.pool, lambda: nc.gpsimd.tensor_tensor(out=u[:], in0=u[:], in1=t[:], op=ALU.add), reads=[u, t], writes=[u])
                sq = sqs.next()
                kb.op(kb.act, lambda: nc.scalar.activation(out=sq[:], in_=u[:], func=AF.Square), reads=[u], writes=[sq])

                def part2():
                    kb.op(kb.pe, lambda: nc.tensor.matmul(ssqb[:], lhsT=ones_bf[:], rhs=sq[:], start=True, stop=True),
                          reads=[sq, ones_bf], writes=[ssqb])
                    rs = rss.next()
                    kb.op(kb.act, lambda: nc.scalar.activation(out=rs[:], in_=ssqb[:], func=AF.Ln, bias=epsb[:, 0:1], scale=1.0 / 128),
                          reads=[ssqb, epsb], writes=[rs])
                    kb.op(kb.act, lambda: nc.scalar.activation(out=rs[:], in_=rs[:], func=AF.Exp, bias=epsb[:, 1:2], scale=-0.5),
                          reads=[rs, epsb], writes=[rs])
                    ost = osts.next()
                    kb.op(kb.dve, lambda: nc.vector.scalar_tensor_tensor(out=ost[:], in0=u[:], scalar=subgc, in1=rs[:],
                                                                        op0=ALU.mult, op1=ALU.mult), reads=[u, prm, rs], writes=[ost])
                    kb.dma(kb.sp, self.odaT[h][:, i * 512:(i + 1) * 512], ost[:], reads=[ost])
                deferred.append((idx + 3, part2))

            n = len(items)
            for idx in range(n + 1):
                if idx < n:
                    qk_exp(items[idx], idx)
                if idx >= 1:
                    pv(items[idx - 1], idx - 1)
                while deferred and deferred[0][0] <= idx:
                    deferred.pop(0)[1]()
            for _, f in deferred:
                f()
            while conv_q:
                conv_q.pop(0)()
            kb.end_phase()

    def phase_B2(self, l):
        kb, nc, S, NB, NT = self.kb, self.nc, self.S, self.NB, self.NT
        with ExitStack() as es:
            T = lambda name, shape, dt=F32, psum=False: kb.tile(es, name, shape, dt, psum)
            prm, cbf, cst = self.load_common(es, l, need_cst=True)
            o0 = 2 * NT * 32
            DmT = cst[:, o0:o0 + 512].rearrange("p (h c) -> p h c", c=128)
            qdec = cst[:, o0 + 512:o0 + 768].rearrange("p (a c) -> p a c", c=128)
            cdcol = cst[:, o0 + 772:o0 + 774]
            rqT = T("rqT", [128, 2, S], BF16)
            rkT = T("rkT", [128, 2, S], BF16)
            qdT = T("qdT", [128, 2, S], BF16)
            rkd = T("rkdl", [128, NT, 256], BF16)
            rv = T("rvl", [128, NT, 256], BF16)
            rgg = T("rggl", [128, NT, 256], F32)
            kb.dma(kb.sp, rqT[:], self.rqT.rearrange("a p s -> p a s"), writes=[rqT], big=True)
            kb.dma(kb.sp, rkT[:], self.rkT.rearrange("a p s -> p a s"), writes=[rkT], big=True)
            for t8 in range(0, NT, 4):
                sl = slice(t8 * 128, (t8 + 4) * 128)
                kb.dma(kb.sp, rkd[:, t8:t8 + 4, :], self.rkd[sl, :].rearrange("(t p) c -> p t c", p=128), writes=[rkd])
                kb.dma(kb.sp, rv[:, t8:t8 + 4, :], self.rv[sl, :].rearrange("(t p) c -> p t c", p=128), writes=[rv])
                kb.dma(kb.sp, rgg[:, t8:t8 + 4, :], self.rgg[sl, :].rearrange("(t p) c -> p t c", p=128), writes=[rgg])
            for cc in range(2):
                kb.op(kb.dve, lambda cc=cc: nc.vector.tensor_tensor(
                    out=qdT[:, cc, :].rearrange("p (t c) -> p t c", c=128), in0=rqT[:, cc, :].rearrange("p (t c) -> p t c", c=128),
                    in1=bc(qdec[:, cc, :], 1, NT), op=ALU.mult), reads=[rqT, cst], writes=[qdT])
            Rf = T("Rf", [128, 2, 64])
            Rbs = Rot([T(f"Rb{i}", [128, 2, 64], BF16) for i in range(2)])
            kb.op(kb.dve, lambda: nc.vector.memset(Rf[:], 0.0), writes=[Rf])
            epsb = T("epsb2", [128, 1])
            kb.op(kb.dve, lambda: nc.vector.memset(epsb[:], EPS), writes=[epsb])
            scps = Rot([(T(f"scpa{i}", [128, 2, 128], F32, True), T(f"scpb{i}", [128, 2, 128], F32, True)) for i in range(1)])
            kvps = Rot([T(f"kvps{i}", [128, 512], F32, True) for i in range(2)])
            ops_ = Rot([T(f"ops{i}", [128, 512], F32, True) for i in range(2)])
            trb = T("trb2", [128, 512], F32, True)
            trv = trb[:].bitcast(BF16).rearrange("p (a t) -> p a t", a=2)
            scms = Rot([T(f"scm{i}", [128, 4, 128], BF16) for i in range(2)])
            sqs = Rot([T(f"rsq{i}", [128, 4, 64]) for i in range(2)])
            sss = Rot([T(f"rss{i}", [128, 8]) for i in range(3)])
            onsr = Rot([T(f"ron{i}", [128, 4, 64]) for i in range(2)])
            ogs = Rot([T(f"rog{i}", [128, 256], BF16) for i in range(4)])
            osts = Rot([T(f"rost{i}", [128, 2, 512], BF16) for i in range(2)])
            Rb = Rbs.next()
            kb.op(kb.dve, lambda: nc.vector.memset(Rb[:], 0.0), writes=[Rb])
            cur = {"Rb": Rb}
            st1 = {}

            def S1(n):
                spa, spb = scps.next()
                for h in range(4):
                    cc, po = h // 2, (h % 2) * 64
                    sp_ = spa if po == 0 else spb
                    kb.op(kb.pe, lambda h=h, cc=cc, po=po, sp_=sp_: nc.tensor.matmul(
                        sp_[:, cc, :], lhsT=rkT[po:po + 64, cc, n * 128:(n + 1) * 128], rhs=rqT[po:po + 64, cc, n * 128:(n + 1) * 128],
                        start=True, stop=True, skip_group_check=True), reads=[rkT, rqT], writes=[sp_])
                scm = scms.next()
                scv = scm[:].rearrange("p (cc hh) c -> p cc hh c", hh=2)
                dmv = DmT.rearrange("p (cc hh) c -> p cc hh c", hh=2)
                kb.op(kb.dve, lambda: nc.vector.tensor_tensor(out=scv[:, :, 0, :], in0=spa[:], in1=dmv[:, :, 0, :], op=ALU.mult),
                      reads=[spa, cst], writes=[scm])
                kb.op(kb.dve, lambda: nc.vector.tensor_tensor(out=scv[:, :, 1, :], in0=spb[:], in1=dmv[:, :, 1, :], op=ALU.mult),
                      reads=[spb, cst], writes=[scm])
                kv = kvps.next()
                kvv = kv[:, 0:128].rearrange("p (a e) -> p a e", e=64)
                for h in range(4):
                    cc, po = h // 2, (h % 2) * 64
                    kb.op(kb.pe, lambda h=h, cc=cc, po=po: nc.tensor.matmul(
                        kvv[po:po + 64, cc, :], lhsT=rkd[:, n, h * 64:(h + 1) * 64], rhs=rv[:, n, h * 64:(h + 1) * 64],
                        start=True, stop=True, skip_group_check=True), reads=[rkd, rv], writes=[kv])
                st1[n] = (scm, kv, kvv)

            pend_tr = []

            def S2(n):
                scm, kv, kvv = st1.pop(n)
                Rb = cur["Rb"]
                op_ = ops_.next()
                ov = op_[:, 0:256].rearrange("p (h e) -> p h e", e=64)
                for h in range(4):
                    cc, po = h // 2, (h % 2) * 64
                    kb.op(kb.pe, lambda h=h: nc.tensor.matmul(ov[:, h, :], lhsT=scm[:, h, :], rhs=rv[:, n, h * 64:(h + 1) * 64],
                                                              start=True, stop=False, skip_group_check=True),
                          reads=[scm, rv], writes=[op_])
                    kb.op(kb.pe, lambda h=h, cc=cc, po=po: nc.tensor.matmul(
                        ov[:, h, :], lhsT=qdT[po:po + 64, cc, n * 128:(n + 1) * 128], rhs=Rb[po:po + 64, cc, :],
                        start=False, stop=True, skip_group_check=True), reads=[qdT, Rb], writes=[op_])
                if n + 1 < NT:
                    for cc in range(2):
                        kb.op(kb.dve, lambda cc=cc: nc.vector.scalar_tensor_tensor(
                            out=Rf[:, cc, :], in0=Rf[:, cc, :], scalar=cdcol[:, cc:cc + 1], in1=kvv[:, cc, :],
                            op0=ALU.mult, op1=ALU.add), reads=[Rf, kv, cst], writes=[Rf])
                    Rn = Rbs.next()
                    kb.op(kb.pool, lambda: nc.gpsimd.tensor_copy(out=Rn[:], in_=Rf[:]), reads=[Rf], writes=[Rn])
                    cur["Rb"] = Rn
                sq = sqs.next()
                ss = sss.next()
                kb.op(kb.act, lambda: nc.scalar.activation(out=sq[:], in_=ov, func=AF.Square), reads=[op_], writes=[sq])
                kb.op(kb.dve, lambda: nc.vector.tensor_reduce(out=ss[:, 0:4], in_=sq[:], axis=AX.X, op=ALU.add), reads=[sq], writes=[ss])
                kb.op(kb.act, lambda: nc.scalar.activation(out=ss[:, 4:8], in_=ss[:, 0:4], func=AF.Ln, bias=epsb[:, 0:1], scale=1.0 / 64),
                      reads=[ss, epsb], writes=[ss])
                kb.op(kb.act, lambda: nc.scalar.activation(out=ss[:, 4:8], in_=ss[:, 4:8], func=AF.Exp, scale=-0.5), reads=[ss], writes=[ss])
                on = onsr.next()
                kb.op(kb.dve, lambda: nc.vector.tensor_tensor(out=on[:], in0=ov, in1=bc(ss[:, 4:8], 2, 64), op=ALU.mult),
                      reads=[op_, ss], writes=[on])
                og = ogs.next()
                kb.op(kb.pool, lambda: nc.gpsimd.tensor_tensor(out=og[:], in0=on[:].rearrange("p h e -> p (h e)"), in1=rgg[:, n, :], op=ALU.mult),
                      reads=[on, rgg], writes=[og])

                def tr():
                    for cc in range(2):
                        kb.op(kb.pe, lambda cc=cc: nc.tensor.transpose(
                            out=trv[:, cc, (n % 4) * 128:(n % 4 + 1) * 128], in_=og[:, cc * 128:(cc + 1) * 128], identity=cbf[:, 0:128]),
                              reads=[og, cbf], writes=[trb])
                    if n % 4 == 3:
                        ost = osts.next()
                        kb.op(kb.act, lambda: nc.scalar.copy(out=ost[:], in_=trv), reads=[trb], writes=[ost])
                        blk = n // 4
                        kb.dma(kb.sp, self.oretT[:, :, blk * 512:(blk + 1) * 512].rearrange("a p t -> p a t"), ost[:], reads=[ost])
                pend_tr.append(tr)

            S1(0)
            for n in range(NT):
                if n + 1 < NT:
                    S1(n + 1)
                while len(pend_tr) > 1:
                    pend_tr.pop(0)()
                S2(n)
            while pend_tr:
                pend_tr.pop(0)()
            kb.end_phase()

    def phase_C(self, l, xsrc):
        kb, nc, S, NB, NT = self.kb, self.nc, self.S, self.NB, self.NT
        with ExitStack() as es:
            T = lambda name, shape, dt=F32, psum=False: kb.tile(es, name, shape, dt, psum)
            self.cast_need(("bd", l), ("br", l), ("bs", l), ("o", l))
            wbd = T("wbd", [128, 4, D_MODEL], BF16)
            wbr = T("wbr", [128, 2, D_MODEL], BF16)
            wbs = T("wbs", [128, 2, D_MODEL], BF16)
            wo = T("wo", [128, 8, D_MODEL], BF16)
            for wt, key, nk in ((wbd, "bd", 4), (wbr, "br", 2), (wbs, "bs", 2), (wo, "o", 8)):
                for kc in range(nk):
                    kb.dma(kb.sp, wt[:, kc, :], self.wb[(key, l)][kc * 128:(kc + 1) * 128, :],
                           reads=[self.wres[(key, l)]], writes=[wt], big=True)
            NBUF = 3
            oda = [T(f"oda{i}", [128, 4, 512], BF16) for i in range(NBUF)]
            ort = [T(f"ort{i}", [128, 2, 512], BF16) for i in range(NBUF)]
            osc = [T(f"osc{i}", [128, 2, 512], BF16) for i in range(NBUF)]
            gt = [T(f"gt{i}", [128, 24, 512], BF16) for i in range(NBUF)]
            xt = [T(f"xtc{i}", [128, 4, D_MODEL]) for i in range(NBUF)]
            ms = [Rot([T(f"m{k}_{i}", [128, 512]) for i in range(3)]) for k in range(3)]
            yT = T("yT", [128, 8, 512], BF16)
            pbr = [[T(f"pb{a}{k}", [128, 512], F32, True) for k in range(3)] for a in range(2)]
            pos = Rot([T(f"po{i}", [128, 512], F32, True) for i in range(2)])

            def load(b):
                i = b % NBUF
                t0 = b * 512
                kb.dma(kb.sp, oda[i][:], self.odaT[:, :, t0:t0 + 512].rearrange("a p t -> p a t"), writes=[oda[i]])
                kb.dma(kb.sp, ort[i][:], self.oretT[:, :, t0:t0 + 512].rearrange("a p t -> p a t"), writes=[ort[i]])
                kb.dma(kb.sp, osc[i][:], self.oscT[:, :, t0:t0 + 512].rearrange("a p t -> p a t"), writes=[osc[i]])
                for g4 in range(0, 24, 12):
                    kb.dma(kb.sp, gt[i][:, g4:g4 + 12, :], self.gates[g4:g4 + 12, :, t0:t0 + 512].rearrange("a p t -> p a t"), writes=[gt[i]])
                kb.dma(kb.sp, xt[i][:], xsrc[t0:t0 + 512, :].rearrange("(s p) d -> p s d", p=128), writes=[xt[i]], big=True)

            load(0)
            if NB > 1:
                load(1)
            for b in range(NB):
                if b + 2 < NB:
                    load(b + 2)
                i = b % NBUF
                t0 = b * 512
                srcs = ((oda[i], wbd, 4), (ort[i], wbr, 2), (osc[i], wbs, 2))
                for dc in range(8):
                    pb = pbr[dc % 2]
                    for k, (ot, wt, nk) in enumerate(srcs):
                        for kc in range(nk):
                            kb.op(kb.pe, lambda k=k, kc=kc, ot=ot, wt=wt, nk=nk: nc.tensor.matmul(
                                pb[k][:], lhsT=wt[:, kc, dc * 128:(dc + 1) * 128], rhs=ot[:, kc, :],
                                start=(kc == 0), stop=(kc == nk - 1)), reads=[ot, wt], writes=[pb[k]])
                    mt = [ms[k].next() for k in range(3)]
                    for k in range(3):
                        kb.op(kb.dve, lambda k=k: nc.vector.tensor_tensor(out=mt[k][:], in0=pb[k][:], in1=gt[i][:, 8 * k + dc, :], op=ALU.mult),
                              reads=[pb[k], gt[i]], writes=[mt[k]])
                    kb.op(kb.pool, lambda: nc.gpsimd.tensor_tensor(out=mt[0][:], in0=mt[0][:], in1=mt[1][:], op=ALU.add),
                          reads=[mt[0], mt[1]], writes=[mt[0]])
                    kb.op(kb.pool, lambda: nc.gpsimd.tensor_tensor(out=yT[:, dc, :], in0=mt[0][:], in1=mt[2][:], op=ALU.add),
                          reads=[mt[0], mt[2]], writes=[yT])
                for s in range(4):
                    for half in range(2):
                        po = pos.next()
                        for kc in range(8):
                            kb.op(kb.pe, lambda kc=kc: nc.tensor.matmul(
                                po[:], lhsT=yT[:, kc, s * 128:(s + 1) * 128], rhs=wo[:, kc, half * 512:(half + 1) * 512],
                                start=(kc == 0), stop=(kc == 7)), reads=[yT, wo], writes=[po])
                        xs = xt[i][:, s, half * 512:(half + 1) * 512]
                        kb.op(kb.dve, lambda: nc.vector.tensor_tensor(out=xs, in0=po[:], in1=xs, op=ALU.add),
                              reads=[po, xt[i]], writes=[xt[i]])
                kb.dma(kb.sp, self.x1[t0:t0 + 512, :].rearrange("(s p) d -> p s d", p=128), xt[i][:], reads=[xt[i]], big=True)
            kb.end_phase()

    def phase_D(self, l, xdst):
        kb, nc, S, NB, NT = self.kb, self.nc, self.S, self.NB, self.NT
        with ExitStack() as es:
            T = lambda name, shape, dt=F32, psum=False: kb.tile(es, name, shape, dt, psum)
            xrs = Rot([T(f"xr{i}", [128, 512]) for i in range(3)])
            prm, cbf, _ = self.load_common(es, l)
            self.cast_need(("fi", l), ("fo", l))
            wfi = T("wfi", [128, 8, 2 * D_FF], BF16)
            wfo = T("wfo", [128, NFC, D_MODEL], BF16)
            for kc in range(8):
                kb.dma(kb.sp, wfi[:, kc, :], self.wb[("fi", l)][kc * 128:(kc + 1) * 128, :], reads=[self.wres[("fi", l)]], writes=[wfi], big=True)
            for f in range(NFC):
                kb.dma(kb.sp, wfo[:, f, :], self.wb[("fo", l)][f * 128:(f + 1) * 128, :], reads=[self.wres[("fo", l)]], writes=[wfo], big=True)
            xt = T("xtd", [128, 4, D_MODEL])
            xb = T("xbd", [128, 4, D_MODEL], BF16)
            hT = T("hTd", [128, 8, 512], BF16)
            actT = T("actT", [128, NFC, 512], BF16)
            small = {"ssq": T("ssqd", [128, 4]), "rstd": T("rstdd", [128, 4]), "eps": T("epsd", [128, 4])}
            kb.op(kb.dve, lambda: nc.vector.memset(small["eps"][:, 0:1], EPS), writes=[small["eps"]])
            Hh = T("halo", [128, NFC, 2])
            kb.op(kb.dve, lambda: nc.vector.memset(Hh[:], 0.0), writes=[Hh])
            gss = Rot([T(f"gs{i}", [128, 514]) for i in range(2)])
            aas = Rot([T(f"aa{i}", [128, 512]) for i in range(2)])
            trp = kb.psum_bf16_views(es, "trpd", 4)
            pgs = Rot([T(f"pg{i}", [128, 512], F32, True) for i in range(2)])
            pus = Rot([T(f"pu{i}", [128, 512], F32, True) for i in range(2)])
            pos = Rot([T(f"pod{i}", [128, 512], F32, True) for i in range(2)])
            def load_x(b):
                kb.dma(kb.sp, xt[:], self.x1[b * 512:(b + 1) * 512, :].rearrange("(s p) d -> p s d", p=128), writes=[xt])

            load_x(0)
            self.rmsnorm_act(xt, xb, small)
            for b in range(NB):
                t0 = b * 512
                self.rmsnorm_tr(l, self.P_FG, xb, hT, trp, prm, cbf)
                if b + 1 < NB:
                    load_x(b + 1)
                stg = {}

                def stage1(f):
                    pg, pu = pgs.next(), pus.next()
                    for kc in range(8):
                        kb.op(kb.pe, lambda kc=kc: nc.tensor.matmul(pg[:], lhsT=wfi[:, kc, f * 128:(f + 1) * 128], rhs=hT[:, kc, :],
                                                                    start=(kc == 0), stop=(kc == 7)), reads=[hT, wfi], writes=[pg])
                    for kc in range(8):
                        kb.op(kb.pe, lambda kc=kc: nc.tensor.matmul(pu[:], lhsT=wfi[:, kc, D_FF + f * 128:D_FF + (f + 1) * 128],
                                                                    rhs=hT[:, kc, :], start=(kc == 0), stop=(kc == 7)),
                              reads=[hT, wfi], writes=[pu])
                    gs, aa = gss.next(), aas.next()
                    wc = lambda k: self.pcol(prm, l, self.P_FCW + f * 3 + k)
                    bcol = self.pcol(prm, l, self.P_FCB + f)
                    kb.op(kb.pool, lambda: nc.gpsimd.tensor_copy(out=gs[:, 0:2], in_=Hh[:, f, :]), reads=[Hh], writes=[gs])
                    kb.op(kb.act, lambda: nc.scalar.copy(out=gs[:, 2:514], in_=pg[:]), reads=[pg], writes=[gs])
                    kb.op(kb.act, lambda: nc.scalar.activation(out=aa[:], in_=pg[:], func=AF.Identity, bias=bcol, scale=wc(2)),
                          reads=[pg, prm], writes=[aa])
                    kb.op(kb.pool, lambda: nc.gpsimd.tensor_copy(out=Hh[:, f, :], in_=gs[:, 512:514]), reads=[gs], writes=[Hh])
                    stg[f] = (pu, gs, aa)

                def stage2(f):
                    pu, gs, aa = stg.pop(f)
                    wc = lambda k: self.pcol(prm, l, self.P_FCW + f * 3 + k)
                    kb.op(kb.dve, lambda: nc.vector.scalar_tensor_tensor(out=aa[:], in0=gs[:, 1:513], scalar=wc(1), in1=aa[:],
                                                                        op0=ALU.mult, op1=ALU.add), reads=[gs, aa, prm], writes=[aa])
                    kb.op(kb.dve, lambda: nc.vector.scalar_tensor_tensor(out=aa[:], in0=gs[:, 0:512], scalar=wc(0), in1=aa[:],
                                                                        op0=ALU.mult, op1=ALU.add), reads=[gs, aa, prm], writes=[aa])
                    kb.op(kb.act, lambda: nc.scalar.activation(out=aa[:], in_=aa[:], func=AF.Silu), reads=[aa], writes=[aa])
                    kb.op(kb.dve, lambda: nc.vector.tensor_tensor(out=actT[:, f, :], in0=aa[:], in1=pu[:], op=ALU.mult),
                          reads=[aa, pu], writes=[actT])

                stage1(0)
                for f in range(NFC):
                    if f + 1 < NFC:
                        stage1(f + 1)
                    stage2(f)
                    if f == 8 and b + 1 < NB:
                        self.rmsnorm_act(xt, xb, small)
                for s in range(4):
                    for half in range(2):
                        po = pos.next()
                        for f in range(NFC):
                            kb.op(kb.pe, lambda f=f: nc.tensor.matmul(
                                po[:], lhsT=actT[:, f, s * 128:(s + 1) * 128], rhs=wfo[:, f, half * 512:(half + 1) * 512],
                                start=(f == 0), stop=(f == NFC - 1)), reads=[actT, wfo], writes=[po])
                        xr = xrs.next()
                        rows = slice(t0 + s * 128, t0 + (s + 1) * 128)
                        cols = slice(half * 512, (half + 1) * 512)
                        kb.dma(kb.sp, xr[:], self.x1[rows, cols], writes=[xr])
                        kb.op(kb.dve, lambda: nc.vector.tensor_tensor(out=xr[:], in0=po[:], in1=xr[:], op=ALU.add),
                              reads=[po, xr], writes=[xr])
                        kb.dma(kb.sp, xdst[rows, cols], xr[:], reads=[xr])
            kb.end_phase()


def rel_bucket_np(n):
    n = np.asarray(n)
    max_exact = 16
    nf = np.maximum(n, 1).astype(np.float32)
    large = max_exact + (np.log(nf / np.float32(max_exact)) / np.float32(math.log(128 / max_exact))
                         * np.float32(32 - max_exact)).astype(np.int32)
    large = np.minimum(large, 31)
    return np.where(n < max_exact, n, large)


def host_consts(S):
    NT = S // 128
    inv = (np.float32(10000.0) ** (-np.arange(32, dtype=np.float32) / np.float32(32))).astype(np.float32)
    ang = (np.arange(S, dtype=np.float32)[:, None] * inv[None, :]).astype(np.float32)
    cos = np.cos(ang).astype(np.float32).reshape(NT, 128, 32).transpose(1, 0, 2).reshape(128, NT * 32)
    sin = np.sin(ang).astype(np.float32).reshape(NT, 128, 32).transpose(1, 0, 2).reshape(128, NT * 32)
    gam = 1.0 - 2.0 ** (-5.0 - np.arange(4, dtype=np.float64))
    idx = np.arange(128, dtype=np.float64)
    diff = idx[None, :] - idx[:, None]
    DmT = np.zeros((128, 4, 128), np.float64)
    for h in range(4):
        DmT[:, h, :] = np.where(diff >= 0, gam[h] ** np.maximum(diff, 0), 0.0)
    qdec = np.zeros((128, 2, 128), np.float64)
    cdcol = np.zeros((128, 2), np.float64)
    for cc in range(2):
        for hh in range(2):
            h = 2 * cc + hh
            qdec[hh * 64:(hh + 1) * 64, cc, :] = gam[h] ** (idx + 1.0)[None, :]
            cdcol[hh * 64:(hh + 1) * 64, cc] = gam[h] ** 128.0
    dk8 = np.zeros((128, 4), np.float64)
    for h in range(4):
        dk8[:, h] = gam[h] ** (127.0 - idx) / 8.0
    cst = np.concatenate([cos, sin, DmT.reshape(128, 512), qdec.reshape(128, 256), dk8, cdcol], axis=1).astype(np.float32)
    ident = np.eye(128, dtype=np.float32)
    blk = np.zeros((128, 128), np.float32)
    blk[:64, :64] = 1.0
    blk[64:, 64:] = 1.0
    cbf = np.concatenate([ident, blk], axis=1).astype(ml_dtypes.bfloat16)
    return np.ascontiguousarray(cst), np.ascontiguousarray(cbf)


def host_params(inp):
    f = lambda a: np.asarray(a, dtype=np.float32)
    prm = np.zeros((128, DEPTH * PRM_L + 4), np.float32)
    for l in range(DEPTH):
        o = l * PRM_L
        prm[:, o + 0:o + 8] = f(inp["norm_mix_g"])[l].reshape(8, 128).T
        prm[:, o + 8:o + 16] = f(inp["norm_ffn_g"])[l].reshape(8, 128).T
        prm[:, o + 16:o + 40] = f(inp["b_gate"])[l].reshape(24, 128).T
        prm[:, o + 40] = np.tile(f(inp["da_q_norm_g"])[l], 2)
        prm[:, o + 41] = np.tile(f(inp["da_k_norm_g"])[l], 2)
        scw = f(inp["sc_conv_w"])[l]
        prm[:, o + 42:o + 48] = scw.reshape(3, 2, 128).transpose(2, 1, 0).reshape(128, 6)
        prm[:, o + 48:o + 50] = f(inp["sc_conv_b"])[l].reshape(2, 128).T
        fcw = f(inp["ffn_conv_w"])[l]
        prm[:, o + 50:o + 116] = fcw.reshape(3, NFC, 128).transpose(2, 1, 0).reshape(128, 66)
        prm[:, o + 116:o + 138] = f(inp["ffn_conv_b"])[l].reshape(NFC, 128).T
        prm[:, o + 138:o + 394] = f(inp["da_lambda"])[l].reshape(1, 256)
        prm[:, o + 394:o + 522] = f(inp["da_subln_g"])[l].reshape(1, 128)
        prm[:, o + 522:o + 586] = f(inp["ret_norm_g"])[l].reshape(1, 64)
        prm[:, o + 586] = f(inp["da_subln_g"])[l]
    rb = f(inp["rel_bias"])
    prm[:, DEPTH * PRM_L:DEPTH * PRM_L + 4] = rb[31][None, :]
    k = np.arange(128)[:, None]
    u = np.arange(1024)[None, :]
    n = u - 384 - k
    bidx = rel_bucket_np(np.maximum(n, 0))
    G = np.empty((128, 4, 1024), np.float32)
    for h in range(4):
        G[:, h, :] = np.where(n >= 0, rb[:, h][bidx], np.float32(NEG))
    return prm, G


_CACHE = {}


def get_prog(S, **kw):
    key = (S, tuple(sorted(kw.items())))
    if key not in _CACHE:
        _CACHE[key] = Prog(S, **kw)
        _CACHE[key].build()
    return _CACHE[key]


def make_in_maps(inp, S, ncores):
    cst, cbf = host_consts(S)
    prm, G = host_params(inp)
    f = lambda a: np.ascontiguousarray(np.asarray(a, dtype=np.float32))
    shared = {
        "w_in": f(inp["w_in"]), "w_branch_da": f(inp["w_branch_da"]), "w_branch_ret": f(inp["w_branch_ret"]),
        "w_branch_sc": f(inp["w_branch_sc"]), "w_out": f(inp["w_out"]), "w_ffn_in": f(inp["w_ffn_in"]),
        "w_ffn_out": f(inp["w_ffn_out"]), "cst": cst, "prm": prm, "cbf": cbf, "gbias": G,
    }
    x = f(inp["x"])
    maps = []
    for c in range(ncores):
        m = dict(shared)
        m["x"] = np.ascontiguousarray(x[c])
        maps.append(m)
    return maps


def kernel(**inputs):
    x = np.asarray(inputs["x"])
    B, S, D = x.shape
    prog = get_prog(S)
    maps = make_in_maps(inputs, S, B)
    res = run_bass_kernel_spmd(prog.nc, maps, core_ids=list(range(B)))
    out = np.stack([np.asarray(r["out"]) for r in res.results], axis=0).astype(np.float32)
    return out
```

```python
import math
from contextlib import ExitStack

import numpy as np
import ml_dtypes

import concourse.bass as bass
import concourse.mybir as mybir
from concourse.bass_utils import run_bass_kernel_spmd

F32 = mybir.dt.float32
BF16 = mybir.dt.bfloat16
AF = mybir.ActivationFunctionType
ALU = mybir.AluOpType
AX = mybir.AxisListType

D_MODEL = 1024
DEPTH = 2
SEQ = 4096
BATCH = 8
IN_WIDTH = 6400
D_FF = 2816
NFC = D_FF // 128
EPS = 1e-6
NEG = -30000.0
PRM_L = 587


class Sem:
    __slots__ = ("h", "count")

    def __init__(self, h):
        self.h = h
        self.count = 0


class Res:
    __slots__ = ("name", "last_w", "readers", "dsem", "psum")

    def __init__(self, name, psum=False):
        self.name = name
        self.last_w = None
        self.readers = {}
        self.dsem = None
        self.psum = psum


class Tl:
    def __init__(self, t, res):
        self.t = t
        self.res = res

    def __getitem__(self, k):
        return self.t[k]


class TlView:
    def __init__(self, ap, res):
        self.ap = ap
        self.res = res

    def __getitem__(self, k):
        return self.ap[k]


class Eng:
    def __init__(self, name, e, sem, is_pe=False):
        self.name = name
        self.e = e
        self.sem = sem
        self.waited = {}
        self.is_pe = is_pe


def _res(x):
    return x.res if isinstance(x, (Tl, TlView)) else x


class KB:
    def __init__(self, S, debug=()):
        self.S = S
        self.debug = set(debug)
        self.nc = bass.Bass("TRN2", target_bir_lowering=False)
        self.gstack = ExitStack()
        self.free_sems = []
        self.nsem = 0
        nc = self.nc
        self.pe = Eng("pe", nc.tensor, self.new_sem(), True)
        self.act = Eng("act", nc.scalar, self.new_sem())
        self.dve = Eng("dve", nc.vector, self.new_sem())
        self.pool = Eng("pool", nc.gpsimd, self.new_sem())
        self.sp = Eng("sp", nc.sync, self.new_sem())
        self.engs = [self.pe, self.act, self.dve, self.pool, self.sp]
        self.phase_sems = []
        self.live_dsems = set()
        self.uid = 0

    def new_sem(self):
        if self.free_sems:
            return self.free_sems.pop()
        self.nsem += 1
        h = self.gstack.enter_context(self.nc.semaphore(f"s{self.nsem}"))
        return Sem(h)

    def tile(self, es, name, shape, dt, psum=False):
        self.uid += 1
        nm = f"{name}_{self.uid}"
        if psum:
            t = es.enter_context(self.nc.psum_tensor(nm, shape, dt))
        else:
            t = es.enter_context(self.nc.sbuf_tensor(nm, shape, dt))
        return Tl(t, Res(nm, psum))

    def psum_bf16_views(self, es, name, n):
        out = []
        for i in range(n):
            if i % 2 == 0:
                bank = self.tile(es, f"{name}b{i // 2}", [128, 512], F32, True)
                v = bank[:].bitcast(BF16)
            out.append(TlView(v[:, (i % 2) * 512:(i % 2 + 1) * 512], bank.res))
        return out

    def dram(self, name, shape, dt):
        kind = "ExternalOutput" if name in self.debug else "Internal"
        return self.nc.dram_tensor(name, shape, dt, kind=kind).ap()

    def _waits(self, E, reads, writes):
        need = {}

        def add(t):
            if t is None:
                return
            sem, val = t
            if E.is_pe and sem is E.sem:
                return
            if need.get(sem, 0) < val:
                need[sem] = val

        for r in reads:
            r = _res(r)
            add(r.last_w)
            if r.psum:
                for sem, val in r.readers.items():
                    if sem is not E.sem:
                        add((sem, val))
        for w in writes:
            w = _res(w)
            add(w.last_w)
            for sem, val in w.readers.items():
                add((sem, val))
        for sem, val in need.items():
            if E.waited.get(sem, 0) >= val:
                continue
            E.e.wait_ge(sem.h, val)
            E.waited[sem] = val

    def _commit(self, tok, reads, writes):
        sem, val = tok
        for r in reads:
            r = _res(r)
            if r.readers.get(sem, 0) < val:
                r.readers[sem] = val
        for w in writes:
            w = _res(w)
            w.last_w = tok
            w.readers = {}

    def op(self, E, fn, reads=(), writes=()):
        self._waits(E, reads, writes)
        ins = fn()
        E.sem.count += 1
        ins.then_inc(E.sem.h, 1)
        self._commit((E.sem, E.sem.count), reads, writes)
        return ins

    def dma(self, E, out, in_, reads=(), writes=(), sem=None, big=False, **kw):
        if E is self.sp and not big:
            E = self.pool
        if E is self.sp:
            n = 1
            apl = list((out if out.tensor.__class__.__name__.startswith("DRam") else in_).ap)
            for st, cnt in apl[:-1]:
                n *= cnt
            self.hw_desc = getattr(self, "hw_desc", 0) + max(n, 128)
        if sem is None:
            owner = None
            for x in list(writes) + list(reads):
                if isinstance(x, Tl):
                    owner = x.res
                    break
            assert owner is not None
            if owner.dsem is None:
                owner.dsem = self.new_sem()
                self.phase_sems.append(owner.dsem)
            sem = owner.dsem
        self._waits(E, reads, writes)
        ins = E.e.dma_start(out=out, in_=in_, **kw)
        sem.count += 16
        ins.then_inc(sem.h, 16)
        self.live_dsems.add(sem)
        self._commit((sem, sem.count), reads, writes)
        return ins

    def barrier(self):
        sp = self.sp
        for sem in list(self.live_dsems):
            if sp.waited.get(sem, 0) < sem.count:
                sp.e.wait_ge(sem.h, sem.count)
                sp.waited[sem] = sem.count
        self.live_dsems = set()
        for F in self.engs:
            if F is sp:
                continue
            if F.sem.count > sp.waited.get(F.sem, 0):
                sp.e.wait_ge(F.sem.h, F.sem.count)
                sp.waited[F.sem] = F.sem.count
        ins = sp.e.nop()
        sp.sem.count += 1
        ins.then_inc(sp.sem.h, 1)
        for F in self.engs:
            if F is sp:
                continue
            F.e.wait_ge(sp.sem.h, sp.sem.count)
            F.waited[sp.sem] = sp.sem.count

    def end_phase(self):
        self.barrier()
        self.free_sems.extend(self.phase_sems)
        self.phase_sems = []


class Rot:
    def __init__(self, tiles):
        self.tiles = tiles
        self.i = 0

    def next(self):
        t = self.tiles[self.i % len(self.tiles)]
        self.i += 1
        return t


def bc(ap, axis, n):
    a = ap.unsqueeze(axis)
    shp = list(a.shape)
    shp[axis] = n
    return a.broadcast_to(shp)


class Prog:
    def __init__(self, S, debug=(), upto=None, nlayers=DEPTH):
        self.S = S
        self.NB = S // 512
        self.NT = S // 128
        self.upto = upto
        self.nlayers = nlayers
        kb = self.kb = KB(S, debug)
        nc = self.nc = kb.nc
        NT = self.NT
        ext = lambda name, shape, dt=F32: nc.dram_tensor(name, shape, dt, kind="ExternalInput").ap()
        self.x = ext("x", [S, D_MODEL])
        self.w_in = ext("w_in", [DEPTH, D_MODEL, IN_WIDTH])
        self.w_bd = ext("w_branch_da", [DEPTH, 512, D_MODEL])
        self.w_br = ext("w_branch_ret", [DEPTH, 256, D_MODEL])
        self.w_bs = ext("w_branch_sc", [DEPTH, 256, D_MODEL])
        self.w_o = ext("w_out", [DEPTH, D_MODEL, D_MODEL])
        self.w_fi = ext("w_ffn_in", [DEPTH, D_MODEL, 2 * D_FF])
        self.w_fo = ext("w_ffn_out", [DEPTH, D_FF, D_MODEL])
        self.NC = 2 * NT * 32 + 512 + 256 + 4 + 2
        self.cst_in = ext("cst", [128, self.NC])
        self.prm_in = ext("prm", [128, DEPTH * PRM_L + 4])
        self.cbf_in = ext("cbf", [128, 256], BF16)
        self.G_in = ext("gbias", [128, 4, 1024])
        self.out = nc.dram_tensor("out", [S, D_MODEL], F32, kind="ExternalOutput").ap()
        d = kb.dram
        self.wb = {}
        for l in range(DEPTH):
            self.wb[("in", l)] = d(f"wb_in{l}", [D_MODEL, IN_WIDTH], BF16)
            self.wb[("bd", l)] = d(f"wb_bd{l}", [512, D_MODEL], BF16)
            self.wb[("br", l)] = d(f"wb_br{l}", [256, D_MODEL], BF16)
            self.wb[("bs", l)] = d(f"wb_bs{l}", [256, D_MODEL], BF16)
            self.wb[("o", l)] = d(f"wb_o{l}", [D_MODEL, D_MODEL], BF16)
            self.wb[("fi", l)] = d(f"wb_fi{l}", [D_MODEL, 2 * D_FF], BF16)
            self.wb[("fo", l)] = d(f"wb_fo{l}", [D_FF, D_MODEL], BF16)
        self.wres = {k: Res(f"w{k}") for k in self.wb}
        self.qT = d("s_qT", [4, 128, S], BF16)
        self.kT = d("s_kT", [4, 128, S], BF16)
        self.vS = d("s_v", [S, 512], BF16)
        self.rqT = d("s_rqT", [2, 128, S], BF16)
        self.rkT = d("s_rkT", [2, 128, S], BF16)
        self.rkd = d("s_rkd", [S, 256], BF16)
        self.rv = d("s_rv", [S, 256], BF16)
        self.rgg = d("s_rgg", [S, 256], F32)
        self.sc = d("s_sc", [3, 2, 128, S], F32)
        self.gates = d("s_gates", [S // 512, 128, 24, 512], BF16)
        self.odaT = d("s_odaT", [4, 128, S], BF16)
        self.oretT = d("s_oretT", [2, 128, S], BF16)
        self.oscT = d("s_oscT", [2, 128, S], BF16)
        self.x1 = d("s_x1", [S, D_MODEL], F32)
        self.x2 = d("s_x2", [S, D_MODEL], F32)

    def build(self):
        kb = self.kb
        self.cast_q = []
        self.cast_enqueue(0)
        self.cast_enqueue(1)
        xin = self.x
        order = ["A", "B1", "B2", "C", "D"]
        stop = False
        for l in range(self.nlayers):
            last = l == self.nlayers - 1
            xout = self.out if last else self.x2
            import os
            skip = os.environ.get("PH_SKIP", "").split(",")
            for ph in order:
                if ph in skip:
                    continue
                if ph == "A":
                    self.phase_A(l, xin)
                elif ph == "B1":
                    self.phase_B1(l)
                elif ph == "B2":
                    self.phase_B2(l)
                elif ph == "C":
                    self.phase_C(l, xin)
                elif ph == "D":
                    self.phase_D(l, xout)
                if self.upto == (l, ph):
                    stop = True
                    break
            if stop:
                break
            xin = self.x2
        kb.barrier()
        kb.gstack.close()
        return self.nc

    def cast_enqueue(self, l):
        kb = self.kb
        if l >= self.nlayers:
            return
        srcs = {"in": self.w_in, "bd": self.w_bd, "br": self.w_br, "bs": self.w_bs, "o": self.w_o,
                "fi": self.w_fi, "fo": self.w_fo}
        for key in ["in", "bd", "br", "bs", "o", "fi", "fo"]:
            src = srcs[key][l]
            dst = self.wb[(key, l)]
            rows, cols = dst.shape
            res = self.wres[(key, l)]
            sem = kb.new_sem()
            ncs = (cols + 3199) // 3200
            cw = cols // ncs
            jobs = []
            for r0 in range(0, rows, 128):
                for ci in range(ncs):
                    jobs.append((r0, ci))
            for n, (r0, ci) in enumerate(jobs):
                def fn(dst=dst, src=src, r0=r0, ci=ci, cw=cw, sem=sem, res=res, last=(n == len(jobs) - 1)):
                    kb.dma(kb.pool, dst[r0:r0 + 128, ci * cw:(ci + 1) * cw], src[r0:r0 + 128, ci * cw:(ci + 1) * cw],
                           reads=(), writes=(), sem=sem, max_dma_last_dim=4096)
                    if last:
                        res.last_w = (sem, sem.count)
                self.cast_q.append(((key, l), fn))

    def cast_issue(self, n):
        for _ in range(min(n, len(self.cast_q))):
            self.cast_q.pop(0)[1]()

    def cast_need(self, *keys):
        while any(k in keys for k, _ in self.cast_q):
            self.cast_q.pop(0)[1]()

    def load_common(self, es, l, need_cst=False):
        kb = self.kb
        prm = kb.tile(es, "prm", [128, DEPTH * PRM_L + 4], F32)
        kb.dma(kb.sp, prm[:], self.prm_in, writes=[prm], big=True)
        cbf = kb.tile(es, "cbf", [128, 256], BF16)
        kb.dma(kb.sp, cbf[:], self.cbf_in, writes=[cbf], big=True)
        cst = None
        if need_cst:
            cst = kb.tile(es, "cst", [128, self.NC], F32)
            kb.dma(kb.sp, cst[:], self.cst_in, writes=[cst], big=True)
        return prm, cbf, cst

    def pcol(self, prm, l, off, n=1):
        o = l * PRM_L + off
        return prm[:, o:o + n]

    P_NG, P_FG, P_BG, P_QG, P_KG, P_SCW, P_SCB, P_FCW, P_FCB, P_LAM, P_SUBG, P_RNG, P_SUBGC = (
        0, 8, 16, 40, 41, 42, 48, 50, 116, 138, 394, 522, 586)

    def rmsnorm_T(self, l, goff, xt, xb, hT, trp, prm, cbf, small, E_evac=None):
        self.rmsnorm_act(xt, xb, small)
        self.rmsnorm_tr(l, goff, xb, hT, trp, prm, cbf)

    def rmsnorm_act(self, xt, xb, small):
        kb = self.kb
        nc = self.nc
        ssq, rstd = small["ssq"], small["rstd"]
        for s in range(4):
            kb.op(kb.act, lambda s=s: nc.scalar.activation(out=xb[:, s, :], in_=xt[:, s, :], func=AF.Square,
                                                           accum_out=ssq[:, s:s + 1]),
                  reads=[xt], writes=[xb, ssq])
        kb.op(kb.act, lambda: nc.scalar.activation(out=rstd[:], in_=ssq[:], func=AF.Ln, bias=small["eps"][:, 0:1],
                                                   scale=1.0 / D_MODEL),
              reads=[ssq, small["eps"]], writes=[rstd])
        kb.op(kb.act, lambda: nc.scalar.activation(out=rstd[:], in_=rstd[:], func=AF.Exp, scale=-0.5),
              reads=[rstd], writes=[rstd])
        for s in range(4):
            kb.op(kb.act, lambda s=s: nc.scalar.activation(out=xb[:, s, :], in_=xt[:, s, :], func=AF.Identity,
                                                           scale=rstd[:, s:s + 1]),
                  reads=[xt, rstd], writes=[xb])

    def rmsnorm_tr(self, l, goff, xb, hT, trp, prm, cbf):
        kb = self.kb
        nc = self.nc
        ident = cbf[:, 0:128]
        for half in range(2):
            for i in range(4):
                c = half * 4 + i
                tp = trp[i]
                for s in range(4):
                    kb.op(kb.pe, lambda s=s, c=c, tp=tp: nc.tensor.transpose(
                        out=tp[:, s * 128:(s + 1) * 128], in_=xb[:, s, c * 128:(c + 1) * 128], identity=ident),
                          reads=[xb, cbf], writes=[tp])
            for i in range(4):
                c = half * 4 + i
                tp = trp[i]
                kb.op(kb.dve, lambda c=c, tp=tp: nc.vector.tensor_scalar(
                    out=hT[:, c, :], in0=tp[:], scalar1=self.pcol(prm, l, goff + c), scalar2=None, op0=ALU.mult),
                      reads=[tp, prm], writes=[hT])

    def phase_A(self, l, xsrc):
        kb, nc, S, NB, NT = self.kb, self.nc, self.S, self.NB, self.NT
        P = self
        with ExitStack() as es:
            T = lambda name, shape, dt=F32, psum=False: kb.tile(es, name, shape, dt, psum)
            prm, cbf, cst = self.load_common(es, l, need_cst=True)
            self.cast_need(("in", l))
            w = T("wA", [128, 8, IN_WIDTH], BF16)
            wsrc = self.wb[("in", l)]
            for kc in range(8):
                kb.dma(kb.sp, w[:, kc, :], wsrc[kc * 128:(kc + 1) * 128, :], reads=[self.wres[("in", l)]], writes=[w], big=True)
            xt = T("xt", [128, 4, D_MODEL])
            xb = T("xb", [128, 4, D_MODEL], BF16)
            hT = T("hT", [128, 8, 512], BF16)
            small = {"ssq": T("ssq", [128, 4]), "rstd": T("rstd", [128, 4]), "eps": T("eps", [128, 4])}
            kb.op(kb.dve, lambda: nc.vector.memset(small["eps"][:, 0:1], EPS), writes=[small["eps"]])
            kb.op(kb.dve, lambda: nc.vector.memset(small["eps"][:, 1:2], 64 * EPS), writes=[small["eps"]])
            mm = Rot([T(f"mm{i}", [128, 512], F32, True) for i in range(3)])
            ssqp = T("ssqp", [128, 512], F32, True)
            trp = kb.psum_bf16_views(es, "trp", 8)
            vst = Rot([T(f"vst{i}", [128, 512], BF16) for i in range(2)])
            tt = [Rot([T(f"t{k}_{i}", [128, 8, 32]) for i in range(2)]) for k in range(4)]
            rot = Rot([T(f"rot{i}", [128, 8, 2, 32]) for i in range(2)])
            rqb = Rot([T(f"rqb{i}", [128, 256], BF16) for i in range(2)])
            rkb = Rot([T(f"rkb{i}", [128, 256], BF16) for i in range(2)])
            rkdt = Rot([T(f"rkd{i}", [128, 4, 64], BF16) for i in range(2)])
            rvb = Rot([T(f"rvb{i}", [128, 256], BF16) for i in range(2)])
            sg = Rot([T(f"sg{i}", [128, 256]) for i in range(2)])
            sg2 = Rot([T(f"sg2{i}", [128, 4, 64]) for i in range(2)])
            rggt = Rot([T(f"rgg{i}", [128, 4, 64]) for i in range(2)])
            trst = Rot([T(f"trst{i}", [128, 512], BF16) for i in range(8)])
            sqt = Rot([T(f"sq{i}", [128, 512], BF16) for i in range(2)])
            rst = Rot([T(f"rs{i}", [128, 512]) for i in range(2)])
            qnt = Rot([T(f"qn{i}", [128, 512], BF16) for i in range(2)])
            scst = Rot([T(f"scst{i}", [128, 2, 512]) for i in range(2)])
            gst = Rot([T(f"gst{i}", [128, 2, 512], BF16) for i in range(3)])
            cosv = cst[:, 0:NT * 32].rearrange("p (t d) -> p t d", d=32)
            sinv = cst[:, NT * 32:2 * NT * 32].rearrange("p (t d) -> p t d", d=32)
            dk8 = cst[:, 2 * NT * 32 + 768:2 * NT * 32 + 772]
            rng = self.pcol(prm, l, self.P_RNG, 64)
            blk1 = cbf[:, 128:256]

            def load_x(b):
                kb.dma(kb.sp, xt[:], xsrc[b * 512:(b + 1) * 512, :].rearrange("(s p) d -> p s d", p=128), writes=[xt])

            load_x(0)
            self.rmsnorm_act(xt, xb, small)
            for b in range(NB):
                t0 = b * 512
                self.cast_issue(7)
                self.rmsnorm_tr(l, self.P_NG, xb, hT, trp[0:4], prm, cbf)
                if b + 1 < NB:
                    load_x(b + 1)
                trq = [trp[4], trp[5], trp[6], trp[7]]
                tr_pend = []
                for s in range(4):
                    tix = b * 4 + s
                    r0 = t0 + s * 128
                    for grp in range(3):
                        ps = mm.next()
                        c0 = 1024 + grp * 512
                        for kc in range(8):
                            kb.op(kb.pe, lambda kc=kc, ps=ps, c0=c0, s=s: nc.tensor.matmul(
                                ps[:], lhsT=hT[:, kc, s * 128:(s + 1) * 128], rhs=w[:, kc, c0:c0 + 512],
                                start=(kc == 0), stop=(kc == 7)), reads=[hT, w], writes=[ps])
                        if grp == 2:
                            while len(tr_pend) > 1:
                                tr_pend.pop(0)()
                        if grp == 0:
                            o = vst.next()
                            kb.op(kb.act, lambda o=o, ps=ps: nc.scalar.copy(out=o[:], in_=ps[:]), reads=[ps], writes=[o])
                            kb.dma(kb.sp, self.vS[r0:r0 + 128, :], o[:], reads=[o])
                        elif grp == 1:
                            pv = ps[:].rearrange("p (h two d) -> p h two d", two=2, d=32)
                            x1, x2 = pv[:, :, 0, :], pv[:, :, 1, :]
                            cb = bc(cosv[:, tix, :], 1, 8)
                            sb = bc(sinv[:, tix, :], 1, 8)
                            t1, t2, t3, t4 = [r.next() for r in tt]
                            ro = rot.next()
                            TT = nc.vector.tensor_tensor
                            kb.op(kb.dve, lambda: TT(out=t1[:], in0=x1, in1=cb, op=ALU.mult), reads=[ps, cst], writes=[t1])
                            kb.op(kb.dve, lambda: TT(out=t2[:], in0=x2, in1=sb, op=ALU.mult), reads=[ps, cst], writes=[t2])
                            kb.op(kb.dve, lambda: TT(out=t3[:], in0=x2, in1=cb, op=ALU.mult), reads=[ps, cst], writes=[t3])
                            kb.op(kb.dve, lambda: TT(out=t4[:], in0=x1, in1=sb, op=ALU.mult), reads=[ps, cst], writes=[t4])
                            kb.op(kb.pool, lambda: nc.gpsimd.tensor_tensor(out=ro[:, :, 0, :], in0=t1[:], in1=t2[:], op=ALU.subtract),
                                  reads=[t1, t2], writes=[ro])
                            kb.op(kb.pool, lambda: nc.gpsimd.tensor_tensor(out=ro[:, :, 1, :], in0=t3[:], in1=t4[:], op=ALU.add),
                                  reads=[t3, t4], writes=[ro])
                            rof = ro[:].rearrange("p h two d -> p (h two d)")
                            qb, kbb, kd = rqb.next(), rkb.next(), rkdt.next()
                            kb.op(kb.act, lambda: nc.scalar.copy(out=qb[:], in_=rof[:, 0:256]), reads=[ro], writes=[qb])
                            kb.op(kb.act, lambda: nc.scalar.mul(out=kbb[:], in_=rof[:, 256:512], mul=0.125), reads=[ro], writes=[kbb])
                            kb.op(kb.dve, lambda: TT(out=kd[:], in0=rof[:, 256:512].rearrange("p (h d) -> p h d", d=64),
                                                     in1=bc(dk8, 2, 64), op=ALU.mult), reads=[ro, cst], writes=[kd])
                            kb.dma(kb.sp, self.rkd[r0:r0 + 128, :], kd[:].rearrange("p h d -> p (h d)"), reads=[kd])
                            def do_tr(s=s, qb=qb, kbb=kbb):
                                ident = cbf[:, 0:128]
                                for cc in range(2):
                                    kb.op(kb.pe, lambda cc=cc: nc.tensor.transpose(
                                        out=trq[cc][:, s * 128:(s + 1) * 128], in_=qb[:, cc * 128:(cc + 1) * 128], identity=ident),
                                          reads=[qb, cbf], writes=[trq[cc]])
                                    kb.op(kb.pe, lambda cc=cc: nc.tensor.transpose(
                                        out=trq[2 + cc][:, s * 128:(s + 1) * 128], in_=kbb[:, cc * 128:(cc + 1) * 128], identity=ident),
                                          reads=[kbb, cbf], writes=[trq[2 + cc]])
                            tr_pend.append(do_tr)
                        else:
                            o = rvb.next()
                            kb.op(kb.act, lambda o=o, ps=ps: nc.scalar.copy(out=o[:], in_=ps[:, 0:256]), reads=[ps], writes=[o])
                            kb.dma(kb.sp, self.rv[r0:r0 + 128, :], o[:], reads=[o])
                            g1, g2, g3 = sg.next(), sg2.next(), rggt.next()
                            kb.op(kb.act, lambda g1=g1, ps=ps: nc.scalar.activation(out=g1[:], in_=ps[:, 256:512], func=AF.Sigmoid),
                                  reads=[ps], writes=[g1])
                            kb.op(kb.dve, lambda g1=g1, g2=g2, ps=ps: nc.vector.tensor_tensor(
                                out=g2[:].rearrange("p h d -> p (h d)"), in0=ps[:, 256:512], in1=g1[:], op=ALU.mult),
                                  reads=[ps, g1], writes=[g2])
                            kb.op(kb.pool, lambda g2=g2, g3=g3: nc.gpsimd.tensor_tensor(
                                out=g3[:], in0=g2[:], in1=bc(rng, 1, 4), op=ALU.mult), reads=[g2, prm], writes=[g3])
                            kb.dma(kb.sp, self.rgg[r0:r0 + 128, :], g3[:].rearrange("p h d -> p (h d)"), reads=[g3])
                if b + 1 < NB:
                    self.rmsnorm_act(xt, xb, small)
                while tr_pend:
                    tr_pend.pop(0)()
                dsts = [self.rqT[0], self.rqT[1], self.rkT[0], self.rkT[1]]
                for i in range(4):
                    o = trst.next()
                    kb.op(kb.dve, lambda o=o, i=i: nc.vector.tensor_copy(out=o[:], in_=trq[i][:]), reads=[trq[i]], writes=[o])
                    kb.dma(kb.sp, dsts[i][:, t0:t0 + 512], o[:], reads=[o])
                pend = []

                def qk_post(ps, ci):
                    isq = ci < 4
                    h = ci % 4
                    sq = sqt.next()
                    kb.op(kb.act, lambda: nc.scalar.activation(out=sq[:], in_=ps[:], func=AF.Square), reads=[ps], writes=[sq])

                    def second():
                        kb.op(kb.pe, lambda: nc.tensor.matmul(ssqp[:], lhsT=blk1, rhs=sq[:], start=True, stop=True),
                              reads=[sq, cbf], writes=[ssqp])
                        rs = rst.next()
                        kb.op(kb.act, lambda: nc.scalar.activation(
                            out=rs[:], in_=ssqp[:], func=AF.Ln, bias=small["eps"][:, 1:2] if isq else small["eps"][:, 0:1],
                            scale=1.0 if isq else 1.0 / 64), reads=[ssqp, small["eps"]], writes=[rs])
                        kb.op(kb.act, lambda: nc.scalar.activation(out=rs[:], in_=rs[:], func=AF.Exp, scale=-0.5),
                              reads=[rs], writes=[rs])
                        qn = qnt.next()
                        gcol = self.pcol(prm, l, self.P_QG if isq else self.P_KG)
                        kb.op(kb.dve, lambda: nc.vector.scalar_tensor_tensor(
                            out=qn[:], in0=ps[:], scalar=gcol, in1=rs[:], op0=ALU.mult, op1=ALU.mult),
                              reads=[ps, rs, prm], writes=[qn])
                        dst = self.qT if isq else self.kT
                        kb.dma(kb.sp, dst[h][:, t0:t0 + 512], qn[:], reads=[qn])
                    return second

                qkc = [("qk", ci, ci * 128) for ci in range(8)]
                scc = [("sc", ci, 2560 + ci * 128) for ci in range(6)]
                chunks = []
                for ci in range(8):
                    chunks.append(qkc[ci])
                    if ci < 6:
                        chunks.append(scc[ci])
                chunks += [("g", ci, 3328 + ci * 128) for ci in range(24)]
                for kind, ci, c0 in chunks:
                    ps = mm.next()
                    for kc in range(8):
                        kb.op(kb.pe, lambda kc=kc, ps=ps, c0=c0: nc.tensor.matmul(
                            ps[:], lhsT=w[:, kc, c0:c0 + 128], rhs=hT[:, kc, :], start=(kc == 0), stop=(kc == 7)),
                              reads=[hT, w], writes=[ps])
                    for f in pend:
                        f()
                    pend = []
                    if kind == "qk":
                        pend.append(qk_post(ps, ci))
                    elif kind == "sc":
                        if ci % 2 == 0:
                            sc_o = scst.next()
                        o = sc_o
                        kb.op(kb.act, lambda o=o, ps=ps, ci=ci: nc.scalar.copy(out=o[:, ci % 2, :], in_=ps[:]), reads=[ps], writes=[o])
                        if ci % 2 == 1:
                            kb.dma(kb.sp, self.sc[ci // 2][:, :, t0:t0 + 512].rearrange("a p t -> p a t"), o[:], reads=[o])
                    else:
                        if ci % 2 == 0:
                            g_o = gst.next()
                        o = g_o
                        kb.op(kb.act, lambda o=o, ps=ps, ci=ci: nc.scalar.activation(
                            out=o[:, ci % 2, :], in_=ps[:], func=AF.Sigmoid, bias=self.pcol(prm, l, self.P_BG + ci)),
                              reads=[ps, prm], writes=[o])
                        if ci % 2 == 1:
                            kb.dma(kb.sp, self.gates[b][:, ci - 1:ci + 1, :], o[:], reads=[o])
                for f in pend:
                    f()
            kb.end_phase()

    def phase_B1(self, l):
        kb, nc, S, NB, NT = self.kb, self.nc, self.S, self.NB, self.NT
        lam_init = 0.8 - 0.6 * math.exp(-0.3 * l)
        with ExitStack() as es:
            T = lambda name, shape, dt=F32, psum=False: kb.tile(es, name, shape, dt, psum)
            prm, cbf, _ = self.load_common(es, l)
            G = T("G", [128, 4, 1024])
            kb.dma(kb.sp, G[:], self.G_in, writes=[G], big=True)
            lamv = self.pcol(prm, l, self.P_LAM, 256)
            ltmp = T("ltmp", [128, 64])
            s12 = T("s12", [128, 2])
            e12 = T("e12", [128, 2])
            nlam = T("nlam", [128, 1])
            epsb = T("epsb", [128, 2])
            kb.op(kb.dve, lambda: nc.vector.memset(epsb[:, 0:1], EPS), writes=[epsb])
            kb.op(kb.dve, lambda: nc.vector.memset(epsb[:, 1:2], math.log(1.0 - lam_init)), writes=[epsb])
            for k in range(2):
                kb.op(kb.dve, lambda k=k: nc.vector.tensor_tensor(out=ltmp[:], in0=lamv[:, 128 * k:128 * k + 64],
                                                                 in1=lamv[:, 128 * k + 64:128 * k + 128], op=ALU.mult),
                      reads=[prm], writes=[ltmp])
                kb.op(kb.dve, lambda k=k: nc.vector.tensor_reduce(out=s12[:, k:k + 1], in_=ltmp[:], axis=AX.X, op=ALU.add),
                      reads=[ltmp], writes=[s12])
            kb.op(kb.act, lambda: nc.scalar.activation(out=e12[:], in_=s12[:], func=AF.Exp), reads=[s12], writes=[e12])
            kb.op(kb.dve, lambda: nc.vector.tensor_tensor(out=nlam[:], in0=e12[:, 1:2], in1=e12[:, 0:1], op=ALU.subtract),
                  reads=[e12], writes=[nlam])
            kb.op(kb.dve, lambda: nc.vector.tensor_scalar(out=nlam[:], in0=nlam[:], scalar1=-lam_init, scalar2=None, op0=ALU.add),
                  reads=[nlam], writes=[nlam])
            tb = T("sc_b", [128, S])
            tc = T("sc_c", [128, S])
            tx = T("sc_x", [128, S])
            ty = T("sc_y", [128, S])
            tob = T("sc_o", [128, S], BF16)
            GP = nc.gpsimd
            conv_q = []
            for cc in range(2):
                wc = lambda k, cc=cc: self.pcol(prm, l, self.P_SCW + cc * 3 + k)
                bcol = self.pcol(prm, l, self.P_SCB + cc)

                def c_load(cc=cc):
                    kb.dma(kb.sp, tb[:], self.sc[0, cc], writes=[tb], big=True)
                    kb.dma(kb.sp, tc[:], self.sc[1, cc], writes=[tc], big=True)
                    kb.dma(kb.sp, tx[:], self.sc[2, cc], writes=[tx], big=True)
                conv_q.append(c_load)
                conv_q.append(lambda: kb.op(kb.pool, lambda: GP.tensor_tensor(out=tc[:], in0=tc[:], in1=tx[:], op=ALU.mult),
                                            reads=[tc, tx], writes=[tc]))
                conv_q.append(lambda wc=wc, bcol=bcol: kb.op(kb.pool, lambda: GP.tensor_scalar(
                    out=ty[:], in0=tc[:], scalar1=wc(2), scalar2=bcol, op0=ALU.mult, op1=ALU.add), reads=[tc, prm], writes=[ty]))
                conv_q.append(lambda wc=wc: kb.op(kb.pool, lambda: GP.tensor_scalar(
                    out=tx[:], in0=tc[:], scalar1=wc(1), scalar2=0.0, op0=ALU.mult, op1=ALU.add), reads=[tc, prm], writes=[tx]))
                conv_q.append(lambda: kb.op(kb.pool, lambda: GP.tensor_tensor(out=ty[:, 1:S], in0=ty[:, 1:S], in1=tx[:, 0:S - 1], op=ALU.add),
                                            reads=[ty, tx], writes=[ty]))
                conv_q.append(lambda wc=wc: kb.op(kb.pool, lambda: GP.tensor_scalar(
                    out=tx[:], in0=tc[:], scalar1=wc(0), scalar2=0.0, op0=ALU.mult, op1=ALU.add), reads=[tc, prm], writes=[tx]))
                conv_q.append(lambda: kb.op(kb.pool, lambda: GP.tensor_tensor(out=ty[:, 2:S], in0=ty[:, 2:S], in1=tx[:, 0:S - 2], op=ALU.add),
                                            reads=[ty, tx], writes=[ty]))
                conv_q.append(lambda: kb.op(kb.pool, lambda: GP.tensor_tensor(out=tob[:], in0=tb[:], in1=ty[:], op=ALU.mult),
                                            reads=[tb, ty], writes=[tob]))
                conv_q.append(lambda cc=cc: kb.dma(kb.sp, self.oscT[cc], tob[:], reads=[tob], big=True))
            qTs = Rot([T(f"qT{i}", [128, S], BF16) for i in range(2)])
            kTs = Rot([T(f"kT{i}", [128, S], BF16) for i in range(2)])
            Vs = Rot([T(f"V{i}", [128, NT, 128], BF16) for i in range(2)])
            ones_bf = T("ones_bf", [128, 128], BF16)
            kb.op(kb.dve, lambda: nc.vector.memset(ones_bf[:], 1.0), writes=[ones_bf])
            STb = Rot([T(f"st{a}", [128, 512], F32, True) for a in range(3)])
            accT = [T(f"accT{m}", [128, 512], F32, True) for m in range(2)]
            lsum = [T(f"lsum{m}", [128, 512], F32, True) for m in range(2)]
            ssqb = T("ssqb", [128, 512], F32, True)
            PTs = Rot([T(f"pt{i}", [128, 512], BF16) for i in range(6)])
            sbs = Rot([T(f"sb{i}", [128, 512]) for i in range(3)])
            rls = [Rot([T(f"rl{m}{i}", [128, 512]) for i in range(2)]) for m in range(2)]
            tts = Rot([T(f"et{i}", [128, 512]) for i in range(2)])
            uus = Rot([T(f"eu{i}", [128, 512]) for i in range(2)])
            sqs = Rot([T(f"esq{i}", [128, 512], BF16) for i in range(2)])
            rss = Rot([T(f"ers{i}", [128, 512]) for i in range(2)])
            osts = Rot([T(f"ost{i}", [128, 512], BF16) for i in range(2)])
            subgc = self.pcol(prm, l, self.P_SUBGC)
            cb0 = DEPTH * PRM_L

            def load_head(h):
                q, k, v = qTs.next(), kTs.next(), Vs.next()
                kb.dma(kb.sp, q[:], self.qT[h], writes=[q], big=True)
                kb.dma(kb.sp, k[:], self.kT[h], writes=[k], big=True)
                for t8 in range(0, NT, 4):
                    kb.dma(kb.sp, v[:, t8:t8 + 4, :],
                           self.vS[t8 * 128:(t8 + 4) * 128, h * 128:(h + 1) * 128].rearrange("(t p) e -> p t e", p=128), writes=[v])
                return q, k, v

            heads = {0: load_head(0)}
            items = []
            for h in range(4):
                for i in range(NB):
                    nj = 4 * i + 4
                    for j in range(nj):
                        items.append((h, i, j, nj))
            state = {}
            deferred = []

            def qk_exp(it, idx):
                h, i, j, nj = it
                q, k, v = heads[h]
                delta = 512 * i - 128 * j
                c0 = max(0, -delta)
                pts = []
                sts = []
                for m in range(2):
                    st = STb.next()
                    kb.op(kb.pe, lambda m=m, st=st: nc.tensor.matmul(
                        st[:, c0:512], lhsT=k[m * 64:(m + 1) * 64, j * 128:(j + 1) * 128],
                        rhs=q[m * 64:(m + 1) * 64, i * 512 + c0:(i + 1) * 512], start=True, stop=True),
                          reads=[q, k], writes=[st])
                    sts.append(st)
                for m in range(2):
                    st = sts[m]
                    pt = PTs.next()
                    if delta <= 128:
                        sb = sbs.next()
                        kb.op(kb.dve, lambda: nc.vector.tensor_tensor(
                            out=sb[:, c0:512], in0=st[:, c0:512], in1=G[:, h, delta + 384 + c0:delta + 896], op=ALU.add),
                              reads=[st, G], writes=[sb])
                        kb.op(kb.act, lambda: nc.scalar.activation(out=pt[:, c0:512], in_=sb[:, c0:512], func=AF.Exp),
                              reads=[sb], writes=[pt])
                    else:
                        kb.op(kb.act, lambda: nc.scalar.activation(
                            out=pt[:], in_=st[:], func=AF.Exp, bias=prm[:, cb0 + h:cb0 + h + 1]),
                              reads=[st, prm], writes=[pt])
                    pts.append(pt)
                state[idx] = pts
                if idx % 8 == 4:
                    self.cast_issue(1)
                if idx % 3 == 1 and conv_q:
                    conv_q.pop(0)()

            def pv(it, idx):
                h, i, j, nj = it
                if i == 0 and j == 0 and h + 1 < 4:
                    heads[h + 1] = load_head(h + 1)
                q, k, v = heads[h]
                pts = state.pop(idx)
                delta = 512 * i - 128 * j
                c0 = max(0, -delta)
                for m in range(2):
                    kb.op(kb.pe, lambda m=m: nc.tensor.matmul(
                        accT[m][:, c0:512], lhsT=v[:, j, :], rhs=pts[m][:, c0:512], start=(j == 0), stop=(j == nj - 1)),
                          reads=[pts[m], v], writes=[accT[m]])
                for m in range(2):
                    kb.op(kb.pe, lambda m=m: nc.tensor.matmul(
                        lsum[m][:, c0:512], lhsT=ones_bf[:], rhs=pts[m][:, c0:512], start=(j == 0), stop=(j == nj - 1)),
                          reads=[pts[m], ones_bf], writes=[lsum[m]])
                if j == nj - 1:
                    epilogue(h, i, idx)

            def epilogue(h, i, idx):
                rl = [rls[m].next() for m in range(2)]
                for m in range(2):
                    kb.op(kb.act, lambda m=m: nc.scalar.activation(out=rl[m][:], in_=lsum[m][:], func=AF.Ln), reads=[lsum[m]], writes=[rl[m]])
                for m in range(2):
                    kb.op(kb.act, lambda m=m: nc.scalar.activation(out=rl[m][:], in_=rl[m][:], func=AF.Exp, scale=-1.0), reads=[rl[m]], writes=[rl[m]])
                t = tts.next()
                u = uus.next()
                kb.op(kb.dve, lambda: nc.vector.scalar_tensor_tensor(out=t[:], in0=accT[1][:], scalar=nlam[:, 0:1], in1=rl[1][:],
                                                                    op0=ALU.mult, op1=ALU.mult), reads=[accT[1], nlam, rl[1]], writes=[t])
                kb.op(kb.dve, lambda: nc.vector.tensor_tensor(out=u[:], in0=accT[0][:], in1=rl[0][:], op=ALU.mult),
                      reads=[accT[0], rl[0]], writes=[u])
                kb.op(kb.pool, lambda: nc.gpsimd.tensor_tensor(out=u[:], in0=u[:], in1=t[:], op=ALU.add), reads=[u, t], writes=[u])
                sq = sqs.next()
                kb.op(kb.act, lambda: nc.scalar.activation(out=sq[:], in_=u[:], func=AF.Square), reads=[u], writes=[sq])

                def part2():
                    kb.op(kb.pe, lambda: nc.tensor.matmul(ssqb[:], lhsT=ones_bf[:], rhs=sq[:], start=True, stop=True),
                          reads=[sq, ones_bf], writes=[ssqb])
                    rs = rss.next()
                    kb.op(kb.act, lambda: nc.scalar.activation(out=rs[:], in_=ssqb[:], func=AF.Ln, bias=epsb[:, 0:1], scale=1.0 / 128),
                          reads=[ssqb, epsb], writes=[rs])
                    kb.op(kb.act, lambda: nc.scalar.activation(out=rs[:], in_=rs[:], func=AF.Exp, bias=epsb[:, 1:2], scale=-0.5),
                          reads=[rs, epsb], writes=[rs])
                    ost = osts.next()
                    kb.op(kb.dve, lambda: nc.vector.scalar_tensor_tensor(out=ost[:], in0=u[:], scalar=subgc, in1=rs[:],
                                                                        op0=ALU.mult, op1=ALU.mult), reads=[u, prm, rs], writes=[ost])
                    kb.dma(kb.sp, self.odaT[h][:, i * 512:(i + 1) * 512], ost[:], reads=[ost])
                deferred.append((idx + 3, part2))

            n = len(items)
            for idx in range(n + 1):
                if idx < n:
                    qk_exp(items[idx], idx)
                if idx >= 1:
                    pv(items[idx - 1], idx - 1)
                while deferred and deferred[0][0] <= idx:
                    deferred.pop(0)[1]()
            for _, f in deferred:
                f()
            while conv_q:
                conv_q.pop(0)()
            kb.end_phase()

    def phase_B2(self, l):
        kb, nc, S, NB, NT = self.kb, self.nc, self.S, self.NB, self.NT
        with ExitStack() as es:
            T = lambda name, shape, dt=F32, psum=False: kb.tile(es, name, shape, dt, psum)
            prm, cbf, cst = self.load_common(es, l, need_cst=True)
            o0 = 2 * NT * 32
            DmT = cst[:, o0:o0 + 512].rearrange("p (h c) -> p h c", c=128)
            qdec = cst[:, o0 + 512:o0 + 768].rearrange("p (a c) -> p a c", c=128)
            cdcol = cst[:, o0 + 772:o0 + 774]
            rqT = T("rqT", [128, 2, S], BF16)
            rkT = T("rkT", [128, 2, S], BF16)
            qdT = T("qdT", [128, 2, S], BF16)
            rkd = T("rkdl", [128, NT, 256], BF16)
            rv = T("rvl", [128, NT, 256], BF16)
            rgg = T("rggl", [128, NT, 256], F32)
            kb.dma(kb.sp, rqT[:], self.rqT.rearrange("a p s -> p a s"), writes=[rqT], big=True)
            kb.dma(kb.sp, rkT[:], self.rkT.rearrange("a p s -> p a s"), writes=[rkT], big=True)
            for t8 in range(0, NT, 4):
                sl = slice(t8 * 128, (t8 + 4) * 128)
                kb.dma(kb.sp, rkd[:, t8:t8 + 4, :], self.rkd[sl, :].rearrange("(t p) c -> p t c", p=128), writes=[rkd])
                kb.dma(kb.sp, rv[:, t8:t8 + 4, :], self.rv[sl, :].rearrange("(t p) c -> p t c", p=128), writes=[rv])
                kb.dma(kb.sp, rgg[:, t8:t8 + 4, :], self.rgg[sl, :].rearrange("(t p) c -> p t c", p=128), writes=[rgg])
            for cc in range(2):
                kb.op(kb.dve, lambda cc=cc: nc.vector.tensor_tensor(
                    out=qdT[:, cc, :].rearrange("p (t c) -> p t c", c=128), in0=rqT[:, cc, :].rearrange("p (t c) -> p t c", c=128),
                    in1=bc(qdec[:, cc, :], 1, NT), op=ALU.mult), reads=[rqT, cst], writes=[qdT])
            Rf = T("Rf", [128, 2, 64])
            Rbs = Rot([T(f"Rb{i}", [128, 2, 64], BF16) for i in range(2)])
            kb.op(kb.dve, lambda: nc.vector.memset(Rf[:], 0.0), writes=[Rf])
            epsb = T("epsb2", [128, 1])
            kb.op(kb.dve, lambda: nc.vector.memset(epsb[:], EPS), writes=[epsb])
            scps = Rot([(T(f"scpa{i}", [128, 2, 128], F32, True), T(f"scpb{i}", [128, 2, 128], F32, True)) for i in range(1)])
            kvps = Rot([T(f"kvps{i}", [128, 512], F32, True) for i in range(2)])
            ops_ = Rot([T(f"ops{i}", [128, 512], F32, True) for i in range(2)])
            trb = T("trb2", [128, 512], F32, True)
            trv = trb[:].bitcast(BF16).rearrange("p (a t) -> p a t", a=2)
            scms = Rot([T(f"scm{i}", [128, 4, 128], BF16) for i in range(2)])
            sqs = Rot([T(f"rsq{i}", [128, 4, 64]) for i in range(2)])
            sss = Rot([T(f"rss{i}", [128, 8]) for i in range(3)])
            onsr = Rot([T(f"ron{i}", [128, 4, 64]) for i in range(2)])
            ogs = Rot([T(f"rog{i}", [128, 256], BF16) for i in range(4)])
            osts = Rot([T(f"rost{i}", [128, 2, 512], BF16) for i in range(2)])
            Rb = Rbs.next()
            kb.op(kb.dve, lambda: nc.vector.memset(Rb[:], 0.0), writes=[Rb])
            cur = {"Rb": Rb}
            st1 = {}

            def S1(n):
                spa, spb = scps.next()
                for h in range(4):
                    cc, po = h // 2, (h % 2) * 64
                    sp_ = spa if po == 0 else spb
                    kb.op(kb.pe, lambda h=h, cc=cc, po=po, sp_=sp_: nc.tensor.matmul(
                        sp_[:, cc, :], lhsT=rkT[po:po + 64, cc, n * 128:(n + 1) * 128], rhs=rqT[po:po + 64, cc, n * 128:(n + 1) * 128],
                        start=True, stop=True, skip_group_check=True), reads=[rkT, rqT], writes=[sp_])
                scm = scms.next()
                scv = scm[:].rearrange("p (cc hh) c -> p cc hh c", hh=2)
                dmv = DmT.rearrange("p (cc hh) c -> p cc hh c", hh=2)
                kb.op(kb.dve, lambda: nc.vector.tensor_tensor(out=scv[:, :, 0, :], in0=spa[:], in1=dmv[:, :, 0, :], op=ALU.mult),
                      reads=[spa, cst], writes=[scm])
                kb.op(kb.dve, lambda: nc.vector.tensor_tensor(out=scv[:, :, 1, :], in0=spb[:], in1=dmv[:, :, 1, :], op=ALU.mult),
                      reads=[spb, cst], writes=[scm])
                kv = kvps.next()
                kvv = kv[:, 0:128].rearrange("p (a e) -> p a e", e=64)
                for h in range(4):
                    cc, po = h // 2, (h % 2) * 64
                    kb.op(kb.pe, lambda h=h, cc=cc, po=po: nc.tensor.matmul(
                        kvv[po:po + 64, cc, :], lhsT=rkd[:, n, h * 64:(h + 1) * 64], rhs=rv[:, n, h * 64:(h + 1) * 64],
                        start=True, stop=True, skip_group_check=True), reads=[rkd, rv], writes=[kv])
                st1[n] = (scm, kv, kvv)

            pend_tr = []

            def S2(n):
                scm, kv, kvv = st1.pop(n)
                Rb = cur["Rb"]
                op_ = ops_.next()
                ov = op_[:, 0:256].rearrange("p (h e) -> p h e", e=64)
                for h in range(4):
                    cc, po = h // 2, (h % 2) * 64
                    kb.op(kb.pe, lambda h=h: nc.tensor.matmul(ov[:, h, :], lhsT=scm[:, h, :], rhs=rv[:, n, h * 64:(h + 1) * 64],
                                                              start=True, stop=False, skip_group_check=True),
                          reads=[scm, rv], writes=[op_])
                    kb.op(kb.pe, lambda h=h, cc=cc, po=po: nc.tensor.matmul(
                        ov[:, h, :], lhsT=qdT[po:po + 64, cc, n * 128:(n + 1) * 128], rhs=Rb[po:po + 64, cc, :],
                        start=False, stop=True, skip_group_check=True), reads=[qdT, Rb], writes=[op_])
                if n + 1 < NT:
                    for cc in range(2):
                        kb.op(kb.dve, lambda cc=cc: nc.vector.scalar_tensor_tensor(
                            out=Rf[:, cc, :], in0=Rf[:, cc, :], scalar=cdcol[:, cc:cc + 1], in1=kvv[:, cc, :],
                            op0=ALU.mult, op1=ALU.add), reads=[Rf, kv, cst], writes=[Rf])
                    Rn = Rbs.next()
                    kb.op(kb.pool, lambda: nc.gpsimd.tensor_copy(out=Rn[:], in_=Rf[:]), reads=[Rf], writes=[Rn])
                    cur["Rb"] = Rn
                sq = sqs.next()
                ss = sss.next()
                kb.op(kb.act, lambda: nc.scalar.activation(out=sq[:], in_=ov, func=AF.Square), reads=[op_], writes=[sq])
                kb.op(kb.dve, lambda: nc.vector.tensor_reduce(out=ss[:, 0:4], in_=sq[:], axis=AX.X, op=ALU.add), reads=[sq], writes=[ss])
                kb.op(kb.act, lambda: nc.scalar.activation(out=ss[:, 4:8], in_=ss[:, 0:4], func=AF.Ln, bias=epsb[:, 0:1], scale=1.0 / 64),
                      reads=[ss, epsb], writes=[ss])
                kb.op(kb.act, lambda: nc.scalar.activation(out=ss[:, 4:8], in_=ss[:, 4:8], func=AF.Exp, scale=-0.5), reads=[ss], writes=[ss])
                on = onsr.next()
                kb.op(kb.dve, lambda: nc.vector.tensor_tensor(out=on[:], in0=ov, in1=bc(ss[:, 4:8], 2, 64), op=ALU.mult),
                      reads=[op_, ss], writes=[on])
                og = ogs.next()
                kb.op(kb.pool, lambda: nc.gpsimd.tensor_tensor(out=og[:], in0=on[:].rearrange("p h e -> p (h e)"), in1=rgg[:, n, :], op=ALU.mult),
                      reads=[on, rgg], writes=[og])

                def tr():
                    for cc in range(2):
                        kb.op(kb.pe, lambda cc=cc: nc.tensor.transpose(
                            out=trv[:, cc, (n % 4) * 128:(n % 4 + 1) * 128], in_=og[:, cc * 128:(cc + 1) * 128], identity=cbf[:, 0:128]),
                              reads=[og, cbf], writes=[trb])
                    if n % 4 == 3:
                        ost = osts.next()
                        kb.op(kb.act, lambda: nc.scalar.copy(out=ost[:], in_=trv), reads=[trb], writes=[ost])
                        blk = n // 4
                        kb.dma(kb.sp, self.oretT[:, :, blk * 512:(blk + 1) * 512].rearrange("a p t -> p a t"), ost[:], reads=[ost])
                pend_tr.append(tr)

            S1(0)
            for n in range(NT):
                if n + 1 < NT:
                    S1(n + 1)
                while len(pend_tr) > 1:
                    pend_tr.pop(0)()
                S2(n)
            while pend_tr:
                pend_tr.pop(0)()
            kb.end_phase()

    def phase_C(self, l, xsrc):
        kb, nc, S, NB, NT = self.kb, self.nc, self.S, self.NB, self.NT
        with ExitStack() as es:
            T = lambda name, shape, dt=F32, psum=False: kb.tile(es, name, shape, dt, psum)
            self.cast_need(("bd", l), ("br", l), ("bs", l), ("o", l))
            wbd = T("wbd", [128, 4, D_MODEL], BF16)
            wbr = T("wbr", [128, 2, D_MODEL], BF16)
            wbs = T("wbs", [128, 2, D_MODEL], BF16)
            wo = T("wo", [128, 8, D_MODEL], BF16)
            for wt, key, nk in ((wbd, "bd", 4), (wbr, "br", 2), (wbs, "bs", 2), (wo, "o", 8)):
                for kc in range(nk):
                    kb.dma(kb.sp, wt[:, kc, :], self.wb[(key, l)][kc * 128:(kc + 1) * 128, :],
                           reads=[self.wres[(key, l)]], writes=[wt], big=True)
            NBUF = 3
            oda = [T(f"oda{i}", [128, 4, 512], BF16) for i in range(NBUF)]
            ort = [T(f"ort{i}", [128, 2, 512], BF16) for i in range(NBUF)]
            osc = [T(f"osc{i}", [128, 2, 512], BF16) for i in range(NBUF)]
            gt = [T(f"gt{i}", [128, 24, 512], BF16) for i in range(NBUF)]
            xt = [T(f"xtc{i}", [128, 4, D_MODEL]) for i in range(NBUF)]
            ms = [Rot([T(f"m{k}_{i}", [128, 512]) for i in range(3)]) for k in range(3)]
            yT = T("yT", [128, 8, 512], BF16)
            pbr = [[T(f"pb{a}{k}", [128, 512], F32, True) for k in range(3)] for a in range(2)]
            pos = Rot([T(f"po{i}", [128, 512], F32, True) for i in range(2)])

            def load(b):
                i = b % NBUF
                t0 = b * 512
                kb.dma(kb.sp, oda[i][:], self.odaT[:, :, t0:t0 + 512].rearrange("a p t -> p a t"), writes=[oda[i]])
                kb.dma(kb.sp, ort[i][:], self.oretT[:, :, t0:t0 + 512].rearrange("a p t -> p a t"), writes=[ort[i]])
                kb.dma(kb.sp, osc[i][:], self.oscT[:, :, t0:t0 + 512].rearrange("a p t -> p a t"), writes=[osc[i]])
                kb.dma(kb.sp, gt[i][:].rearrange("p a t -> p (a t)"), self.gates[b].rearrange("p a t -> p (a t)"), writes=[gt[i]])
                kb.dma(kb.sp, xt[i][:], xsrc[t0:t0 + 512, :].rearrange("(s p) d -> p s d", p=128), writes=[xt[i]], big=True)

            load(0)
            if NB > 1:
                load(1)
            for b in range(NB):
                if b + 2 < NB:
                    load(b + 2)
                i = b % NBUF
                t0 = b * 512
                srcs = ((oda[i], wbd, 4), (ort[i], wbr, 2), (osc[i], wbs, 2))
                for dc in range(8):
                    pb = pbr[dc % 2]
                    for k, (ot, wt, nk) in enumerate(srcs):
                        for kc in range(nk):
                            kb.op(kb.pe, lambda k=k, kc=kc, ot=ot, wt=wt, nk=nk: nc.tensor.matmul(
                                pb[k][:], lhsT=wt[:, kc, dc * 128:(dc + 1) * 128], rhs=ot[:, kc, :],
                                start=(kc == 0), stop=(kc == nk - 1)), reads=[ot, wt], writes=[pb[k]])
                    mt = [ms[k].next() for k in range(3)]
                    for k in range(3):
                        kb.op(kb.dve, lambda k=k: nc.vector.tensor_tensor(out=mt[k][:], in0=pb[k][:], in1=gt[i][:, 8 * k + dc, :], op=ALU.mult),
                              reads=[pb[k], gt[i]], writes=[mt[k]])
                    kb.op(kb.pool, lambda: nc.gpsimd.tensor_tensor(out=mt[0][:], in0=mt[0][:], in1=mt[1][:], op=ALU.add),
                          reads=[mt[0], mt[1]], writes=[mt[0]])
                    kb.op(kb.pool, lambda: nc.gpsimd.tensor_tensor(out=yT[:, dc, :], in0=mt[0][:], in1=mt[2][:], op=ALU.add),
                          reads=[mt[0], mt[2]], writes=[yT])
                for s in range(4):
                    for half in range(2):
                        po = pos.next()
                        for kc in range(8):
                            kb.op(kb.pe, lambda kc=kc: nc.tensor.matmul(
                                po[:], lhsT=yT[:, kc, s * 128:(s + 1) * 128], rhs=wo[:, kc, half * 512:(half + 1) * 512],
                                start=(kc == 0), stop=(kc == 7)), reads=[yT, wo], writes=[po])
                        xs = xt[i][:, s, half * 512:(half + 1) * 512]
                        kb.op(kb.dve, lambda: nc.vector.tensor_tensor(out=xs, in0=po[:], in1=xs, op=ALU.add),
                              reads=[po, xt[i]], writes=[xt[i]])
                kb.dma(kb.sp, self.x1[t0:t0 + 512, :].rearrange("(s p) d -> p s d", p=128), xt[i][:], reads=[xt[i]], big=True)
            kb.end_phase()

    def phase_D(self, l, xdst):
        kb, nc, S, NB, NT = self.kb, self.nc, self.S, self.NB, self.NT
        with ExitStack() as es:
            T = lambda name, shape, dt=F32, psum=False: kb.tile(es, name, shape, dt, psum)
            xrs = Rot([T(f"xr{i}", [128, 512]) for i in range(3)])
            prm, cbf, _ = self.load_common(es, l)
            self.cast_need(("fi", l), ("fo", l))
            wfi = T("wfi", [128, 8, 2 * D_FF], BF16)
            wfo = T("wfo", [128, NFC, D_MODEL], BF16)
            for kc in range(8):
                kb.dma(kb.sp, wfi[:, kc, :], self.wb[("fi", l)][kc * 128:(kc + 1) * 128, :], reads=[self.wres[("fi", l)]], writes=[wfi], big=True)
            for f in range(NFC):
                kb.dma(kb.sp, wfo[:, f, :], self.wb[("fo", l)][f * 128:(f + 1) * 128, :], reads=[self.wres[("fo", l)]], writes=[wfo], big=True)
            xt = T("xtd", [128, 4, D_MODEL])
            xb = T("xbd", [128, 4, D_MODEL], BF16)
            hT = T("hTd", [128, 8, 512], BF16)
            actT = T("actT", [128, NFC, 512], BF16)
            small = {"ssq": T("ssqd", [128, 4]), "rstd": T("rstdd", [128, 4]), "eps": T("epsd", [128, 4])}
            kb.op(kb.dve, lambda: nc.vector.memset(small["eps"][:, 0:1], EPS), writes=[small["eps"]])
            Hh = T("halo", [128, NFC, 2])
            kb.op(kb.dve, lambda: nc.vector.memset(Hh[:], 0.0), writes=[Hh])
            gss = Rot([T(f"gs{i}", [128, 514]) for i in range(2)])
            aas = Rot([T(f"aa{i}", [128, 512]) for i in range(2)])
            trp = kb.psum_bf16_views(es, "trpd", 4)
            pgs = Rot([T(f"pg{i}", [128, 512], F32, True) for i in range(2)])
            pus = Rot([T(f"pu{i}", [128, 512], F32, True) for i in range(2)])
            pos = Rot([T(f"pod{i}", [128, 512], F32, True) for i in range(2)])
            def load_x(b):
                kb.dma(kb.sp, xt[:], self.x1[b * 512:(b + 1) * 512, :].rearrange("(s p) d -> p s d", p=128), writes=[xt])

            load_x(0)
            self.rmsnorm_act(xt, xb, small)
            for b in range(NB):
                t0 = b * 512
                self.rmsnorm_tr(l, self.P_FG, xb, hT, trp, prm, cbf)
                if b + 1 < NB:
                    load_x(b + 1)
                stg = {}

                def stage1(f):
                    pg, pu = pgs.next(), pus.next()
                    for kc in range(8):
                        kb.op(kb.pe, lambda kc=kc: nc.tensor.matmul(pg[:], lhsT=wfi[:, kc, f * 128:(f + 1) * 128], rhs=hT[:, kc, :],
                                                                    start=(kc == 0), stop=(kc == 7)), reads=[hT, wfi], writes=[pg])
                    for kc in range(8):
                        kb.op(kb.pe, lambda kc=kc: nc.tensor.matmul(pu[:], lhsT=wfi[:, kc, D_FF + f * 128:D_FF + (f + 1) * 128],
                                                                    rhs=hT[:, kc, :], start=(kc == 0), stop=(kc == 7)),
                              reads=[hT, wfi], writes=[pu])
                    gs, aa = gss.next(), aas.next()
                    wc = lambda k: self.pcol(prm, l, self.P_FCW + f * 3 + k)
                    bcol = self.pcol(prm, l, self.P_FCB + f)
                    kb.op(kb.pool, lambda: nc.gpsimd.tensor_copy(out=gs[:, 0:2], in_=Hh[:, f, :]), reads=[Hh], writes=[gs])
                    kb.op(kb.act, lambda: nc.scalar.copy(out=gs[:, 2:514], in_=pg[:]), reads=[pg], writes=[gs])
                    kb.op(kb.act, lambda: nc.scalar.activation(out=aa[:], in_=pg[:], func=AF.Identity, bias=bcol, scale=wc(2)),
                          reads=[pg, prm], writes=[aa])
                    kb.op(kb.pool, lambda: nc.gpsimd.tensor_copy(out=Hh[:, f, :], in_=gs[:, 512:514]), reads=[gs], writes=[Hh])
                    stg[f] = (pu, gs, aa)

                def stage2(f):
                    pu, gs, aa = stg.pop(f)
                    wc = lambda k: self.pcol(prm, l, self.P_FCW + f * 3 + k)
                    kb.op(kb.dve, lambda: nc.vector.scalar_tensor_tensor(out=aa[:], in0=gs[:, 1:513], scalar=wc(1), in1=aa[:],
                                                                        op0=ALU.mult, op1=ALU.add), reads=[gs, aa, prm], writes=[aa])
                    kb.op(kb.dve, lambda: nc.vector.scalar_tensor_tensor(out=aa[:], in0=gs[:, 0:512], scalar=wc(0), in1=aa[:],
                                                                        op0=ALU.mult, op1=ALU.add), reads=[gs, aa, prm], writes=[aa])
                    kb.op(kb.act, lambda: nc.scalar.activation(out=aa[:], in_=aa[:], func=AF.Silu), reads=[aa], writes=[aa])
                    kb.op(kb.dve, lambda: nc.vector.tensor_tensor(out=actT[:, f, :], in0=aa[:], in1=pu[:], op=ALU.mult),
                          reads=[aa, pu], writes=[actT])

                stage1(0)
                for f in range(NFC):
                    if f + 1 < NFC:
                        stage1(f + 1)
                    stage2(f)
                    if f == 8 and b + 1 < NB:
                        self.rmsnorm_act(xt, xb, small)
                for s in range(4):
                    for half in range(2):
                        po = pos.next()
                        for f in range(NFC):
                            kb.op(kb.pe, lambda f=f: nc.tensor.matmul(
                                po[:], lhsT=actT[:, f, s * 128:(s + 1) * 128], rhs=wfo[:, f, half * 512:(half + 1) * 512],
                                start=(f == 0), stop=(f == NFC - 1)), reads=[actT, wfo], writes=[po])
                        xr = xrs.next()
                        rows = slice(t0 + s * 128, t0 + (s + 1) * 128)
                        cols = slice(half * 512, (half + 1) * 512)
                        kb.dma(kb.sp, xr[:], self.x1[rows, cols], writes=[xr])
                        kb.op(kb.dve, lambda: nc.vector.tensor_tensor(out=xr[:], in0=po[:], in1=xr[:], op=ALU.add),
                              reads=[po, xr], writes=[xr])
                        kb.dma(kb.sp, xdst[rows, cols], xr[:], reads=[xr])
            kb.end_phase()


def rel_bucket_np(n):
    n = np.asarray(n)
    max_exact = 16
    nf = np.maximum(n, 1).astype(np.float32)
    large = max_exact + (np.log(nf / np.float32(max_exact)) / np.float32(math.log(128 / max_exact))
                         * np.float32(32 - max_exact)).astype(np.int32)
    large = np.minimum(large, 31)
    return np.where(n < max_exact, n, large)


def host_consts(S):
    NT = S // 128
    inv = (np.float32(10000.0) ** (-np.arange(32, dtype=np.float32) / np.float32(32))).astype(np.float32)
    ang = (np.arange(S, dtype=np.float32)[:, None] * inv[None, :]).astype(np.float32)
    cos = np.cos(ang).astype(np.float32).reshape(NT, 128, 32).transpose(1, 0, 2).reshape(128, NT * 32)
    sin = np.sin(ang).astype(np.float32).reshape(NT, 128, 32).transpose(1, 0, 2).reshape(128, NT * 32)
    gam = 1.0 - 2.0 ** (-5.0 - np.arange(4, dtype=np.float64))
    idx = np.arange(128, dtype=np.float64)
    diff = idx[None, :] - idx[:, None]
    DmT = np.zeros((128, 4, 128), np.float64)
    for h in range(4):
        DmT[:, h, :] = np.where(diff >= 0, gam[h] ** np.maximum(diff, 0), 0.0)
    qdec = np.zeros((128, 2, 128), np.float64)
    cdcol = np.zeros((128, 2), np.float64)
    for cc in range(2):
        for hh in range(2):
            h = 2 * cc + hh
            qdec[hh * 64:(hh + 1) * 64, cc, :] = gam[h] ** (idx + 1.0)[None, :]
            cdcol[hh * 64:(hh + 1) * 64, cc] = gam[h] ** 128.0
    dk8 = np.zeros((128, 4), np.float64)
    for h in range(4):
        dk8[:, h] = gam[h] ** (127.0 - idx) / 8.0
    cst = np.concatenate([cos, sin, DmT.reshape(128, 512), qdec.reshape(128, 256), dk8, cdcol], axis=1).astype(np.float32)
    ident = np.eye(128, dtype=np.float32)
    blk = np.zeros((128, 128), np.float32)
    blk[:64, :64] = 1.0
    blk[64:, 64:] = 1.0
    cbf = np.concatenate([ident, blk], axis=1).astype(ml_dtypes.bfloat16)
    return np.ascontiguousarray(cst), np.ascontiguousarray(cbf)


def host_params(inp):
    f = lambda a: np.asarray(a, dtype=np.float32)
    prm = np.zeros((128, DEPTH * PRM_L + 4), np.float32)
    for l in range(DEPTH):
        o = l * PRM_L
        prm[:, o + 0:o + 8] = f(inp["norm_mix_g"])[l].reshape(8, 128).T
        prm[:, o + 8:o + 16] = f(inp["norm_ffn_g"])[l].reshape(8, 128).T
        prm[:, o + 16:o + 40] = f(inp["b_gate"])[l].reshape(24, 128).T
        prm[:, o + 40] = np.tile(f(inp["da_q_norm_g"])[l], 2)
        prm[:, o + 41] = np.tile(f(inp["da_k_norm_g"])[l], 2)
        scw = f(inp["sc_conv_w"])[l]
        prm[:, o + 42:o + 48] = scw.reshape(3, 2, 128).transpose(2, 1, 0).reshape(128, 6)
        prm[:, o + 48:o + 50] = f(inp["sc_conv_b"])[l].reshape(2, 128).T
        fcw = f(inp["ffn_conv_w"])[l]
        prm[:, o + 50:o + 116] = fcw.reshape(3, NFC, 128).transpose(2, 1, 0).reshape(128, 66)
        prm[:, o + 116:o + 138] = f(inp["ffn_conv_b"])[l].reshape(NFC, 128).T
        prm[:, o + 138:o + 394] = f(inp["da_lambda"])[l].reshape(1, 256)
        prm[:, o + 394:o + 522] = f(inp["da_subln_g"])[l].reshape(1, 128)
        prm[:, o + 522:o + 586] = f(inp["ret_norm_g"])[l].reshape(1, 64)
        prm[:, o + 586] = f(inp["da_subln_g"])[l]
    rb = f(inp["rel_bias"])
    prm[:, DEPTH * PRM_L:DEPTH * PRM_L + 4] = rb[31][None, :]
    k = np.arange(128)[:, None]
    u = np.arange(1024)[None, :]
    n = u - 384 - k
    bidx = rel_bucket_np(np.maximum(n, 0))
    G = np.empty((128, 4, 1024), np.float32)
    for h in range(4):
        G[:, h, :] = np.where(n >= 0, rb[:, h][bidx], np.float32(NEG))
    return prm, G


_CACHE = {}


def get_prog(S, **kw):
    key = (S, tuple(sorted(kw.items())))
    if key not in _CACHE:
        _CACHE[key] = Prog(S, **kw)
        _CACHE[key].build()
    return _CACHE[key]


def make_in_maps(inp, S, ncores):
    cst, cbf = host_consts(S)
    prm, G = host_params(inp)
    f = lambda a: np.ascontiguousarray(np.asarray(a, dtype=np.float32))
    shared = {
        "w_in": f(inp["w_in"]), "w_branch_da": f(inp["w_branch_da"]), "w_branch_ret": f(inp["w_branch_ret"]),
        "w_branch_sc": f(inp["w_branch_sc"]), "w_out": f(inp["w_out"]), "w_ffn_in": f(inp["w_ffn_in"]),
        "w_ffn_out": f(inp["w_ffn_out"]), "cst": cst, "prm": prm, "cbf": cbf, "gbias": G,
    }
    x = f(inp["x"])
    maps = []
    for c in range(ncores):
        m = dict(shared)
        m["x"] = np.ascontiguousarray(x[c])
        maps.append(m)
    return maps


def kernel(**inputs):
    x = np.asarray(inputs["x"])
    B, S, D = x.shape
    prog = get_prog(S)
    maps = make_in_maps(inputs, S, B)
    res = run_bass_kernel_spmd(prog.nc, maps, core_ids=list(range(B)))
    out = np.stack([np.asarray(r["out"]) for r in res.results], axis=0).astype(np.float32)
    return out
```
